# Optimizing a Trainium2 kernel written in Bass

```python
import jax, jax.numpy as jnp
from jax import lax
import numpy as np

D_MODEL = 2048
BATCH = 2
SEQ = 4096
DEPTH = 4

D_A = D_MODEL // 2
D_B = D_MODEL // 2
D_C = D_MODEL // 2
GROUP = 128
N_GROUPS_B = D_B // GROUP
CHUNK = 128
CONV_A = 3
CONV_C = 31
N_BRANCH = 3
D_FF = -(-8 * D_MODEL // (3 * 256)) * 256
LN_EPS = 1e-5
DEEPNORM_ALPHA = (2 * DEPTH) ** 0.25
DEEPNORM_BETA = (8 * DEPTH) ** -0.25
D_IN = 3 * D_A + 2 * D_B + 2 * D_C + N_BRANCH * D_MODEL
SPLITS = (D_A, 2 * D_A, 3 * D_A, 3 * D_A + D_B, 3 * D_A + 2 * D_B,
          3 * D_A + 2 * D_B + D_C, 3 * D_A + 2 * D_B + 2 * D_C)

kernel_name = "hybrid_gated_conv_sgu_conformer_deepnorm"


def layer_norm(x, g, b):
    xf = x.astype(jnp.float32)
    mu = jnp.mean(xf, axis=-1, keepdims=True)
    var = jnp.mean(jnp.square(xf - mu), axis=-1, keepdims=True)
    y = (xf - mu) * lax.rsqrt(var + LN_EPS)
    return (y * g.astype(jnp.float32) + b.astype(jnp.float32)).astype(x.dtype)


def causal_depthwise_conv(x, w):
    k, c = w.shape
    return lax.conv_general_dilated(
        x, w[:, None, :].astype(x.dtype), window_strides=(1,), padding=[(k - 1, 0)],
        dimension_numbers=('NWC', 'WIO', 'NWC'), feature_group_count=c)


def short_gated_conv(b_gate, c_gate, h, conv_w):
    return b_gate * causal_depthwise_conv(c_gate * h, conv_w)


def chunked_spatial_gating(u, v, ln_g, ln_b, w_s, b_s):
    bsz, s, _ = v.shape
    n_chunks = s // CHUNK
    v = layer_norm(v, ln_g, ln_b).reshape(bsz, n_chunks, CHUNK, N_GROUPS_B, GROUP)
    causal = jnp.tril(jnp.ones((CHUNK, CHUNK), dtype=bool))
    w = jnp.where(causal, w_s, 0).astype(v.dtype)
    mixed = jnp.einsum('gts,bnsgd->bntgd', w, v) + b_s.T.astype(v.dtype)[:, :, None]
    return u * mixed.reshape(bsz, s, D_B)


def conformer_conv(a, gate, conv_w, conv_b, ln_g, ln_b):
    y = a * jax.nn.sigmoid(gate)
    y = causal_depthwise_conv(y, conv_w) + conv_b
    return jax.nn.silu(layer_norm(y, ln_g, ln_b))


def setup_inputs(seed: int = 0) -> dict:
    key = jax.random.key(seed)
    ks = jax.random.split(key, 24)

    def nrm(k, shape, scale):
        return jax.random.normal(k, shape, jnp.float32) * scale

    L = DEPTH
    return {
        "x": nrm(ks[0], (BATCH, SEQ, D_MODEL), 1.0),
        "ln_in_g": 1.0 + nrm(ks[1], (D_MODEL,), 0.02),
        "ln_in_b": nrm(ks[2], (D_MODEL,), 0.02),
        "w_in": nrm(ks[3], (L, D_MODEL, D_IN), D_MODEL ** -0.5),
        "gate_bias": nrm(ks[4], (L, N_BRANCH * D_MODEL), 0.02),
        "conv_a_w": nrm(ks[5], (L, CONV_A, D_A), CONV_A ** -0.5),
        "sg_ln_g": 1.0 + nrm(ks[6], (L, D_B), 0.02),
        "sg_ln_b": nrm(ks[7], (L, D_B), 0.02),
        "sg_w": nrm(ks[8], (L, N_GROUPS_B, CHUNK, CHUNK), CHUNK ** -0.5),
        "sg_b": 1.0 + nrm(ks[9], (L, N_GROUPS_B, CHUNK), 0.02),
        "cc_conv_w": nrm(ks[10], (L, CONV_C, D_C), CONV_C ** -0.5),
        "cc_conv_b": nrm(ks[11], (L, D_C), 0.02),
        "cc_ln_g": 1.0 + nrm(ks[12], (L, D_C), 0.02),
        "cc_ln_b": nrm(ks[13], (L, D_C), 0.02),
        "w_branch": nrm(ks[14], (L, N_BRANCH, D_A, D_MODEL), D_A ** -0.5),
        "w_out": nrm(ks[15], (L, D_MODEL, D_MODEL), DEEPNORM_BETA * D_MODEL ** -0.5),
        "ln_mix_g": 1.0 + nrm(ks[16], (L, D_MODEL), 0.02),
        "ln_mix_b": nrm(ks[17], (L, D_MODEL), 0.02),
        "w_ffn_in": nrm(ks[18], (L, D_MODEL, 2 * D_FF), D_MODEL ** -0.5),
        "w_ffn_out": nrm(ks[19], (L, D_FF, D_MODEL), DEEPNORM_BETA * D_FF ** -0.5),
        "ln_ffn_g": 1.0 + nrm(ks[20], (L, D_MODEL), 0.02),
        "ln_ffn_b": nrm(ks[21], (L, D_MODEL), 0.02),
    }


def reference(x, ln_in_g, ln_in_b, w_in, gate_bias, conv_a_w, sg_ln_g, sg_ln_b, sg_w, sg_b,
              cc_conv_w, cc_conv_b, cc_ln_g, cc_ln_b, w_branch, w_out, ln_mix_g, ln_mix_b,
              w_ffn_in, w_ffn_out, ln_ffn_g, ln_ffn_b):
    bsz, s, _ = x.shape
    x = layer_norm(x, ln_in_g, ln_in_b)
    for l in range(DEPTH):
        z = jnp.einsum('bsd,de->bse', x, w_in[l])
        a_b, a_c, a_h, b_u, b_v, c_a, c_g, g = jnp.split(z, SPLITS, axis=-1)
        gates = jax.nn.sigmoid(g + gate_bias[l]).reshape(bsz, s, N_BRANCH, D_MODEL)
        y_a = short_gated_conv(a_b, a_c, a_h, conv_a_w[l])
        y_b = chunked_spatial_gating(b_u, b_v, sg_ln_g[l], sg_ln_b[l], sg_w[l], sg_b[l])
        y_c = conformer_conv(c_a, c_g, cc_conv_w[l], cc_conv_b[l], cc_ln_g[l], cc_ln_b[l])
        ys = jnp.stack([y_a, y_b, y_c], axis=2)
        proj = jnp.einsum('bsnc,ncd->bsnd', ys, w_branch[l])
        merged = jnp.sum(gates * proj, axis=2)
        mix = jnp.einsum('bsd,de->bse', merged, w_out[l])
        x = layer_norm(DEEPNORM_ALPHA * x + mix, ln_mix_g[l], ln_mix_b[l])
        h_gate, h_up = jnp.split(jnp.einsum('bsd,df->bsf', x, w_ffn_in[l]), 2, axis=-1)
        ffn = jnp.einsum('bsf,fd->bsd', jax.nn.silu(h_gate) * h_up, w_ffn_out[l])
        x = layer_norm(DEEPNORM_ALPHA * x + ffn, ln_ffn_g[l], ln_ffn_b[l])
    return x
```

```python
import numpy as np
import concourse.bass as bass
import concourse.mybir as mybir
from concourse.bass_utils import run_bass_kernel_spmd
from contextlib import ExitStack

F32 = mybir.dt.float32
BF16 = mybir.dt.bfloat16
AF = mybir.ActivationFunctionType
ALU = mybir.AluOpType

D_MODEL = 2048
SEQ = 4096
DEPTH = 4
D_FF = 5632
LN_EPS = 1e-5
ALPHA = float((2 * DEPTH) ** 0.25)
NCORES = 8
OWN = 1024
HALO = 256
TL = OWN + HALO
GT = 640
S_L = (64, 96, 128, 224)
S_V = (0, 128, 128, 256)
NS = 6
NSCR = 8
ENGS = ("pe", "act", "dve", "pool", "sp")


class Buf:
    __slots__ = ("name", "writer", "readers")

    def __init__(self, name):
        self.name = name
        self.writer = None
        self.readers = {}


class Op:
    __slots__ = ("eng", "fn", "deps", "sig", "idx", "dkey", "dval")

    def __init__(self, eng, fn):
        self.eng = eng
        self.fn = fn
        self.deps = []
        self.sig = False
        self.idx = 0
        self.dkey = None
        self.dval = 0


class Prog:
    def __init__(self, nc):
        self.nc = nc
        self.q = {e: [] for e in ENGS}
        self.dma_last = {}
        self.dma_cnt = {}

    def op(self, eng, fn, reads=(), writes=(), dkey=None):
        o = Op(eng, fn)
        o.dkey = dkey
        deps = {}

        def add(p, raw):
            if p is None or p is o:
                return
            if p.dkey is None and o.dkey is None and p.eng == eng:
                if eng == "pe" or not raw:
                    return
            deps[id(p)] = p

        for b in reads:
            add(b.writer, True)
        for b in writes:
            add(b.writer, True)
            for r in b.readers.values():
                add(r, False)
        if dkey is not None:
            add(self.dma_last.get(dkey), True)
            self.dma_last[dkey] = o
            self.dma_cnt[dkey] = self.dma_cnt.get(dkey, 0) + 1
            o.dval = 16 * self.dma_cnt[dkey]
        for p in deps.values():
            p.sig = True
        o.deps = list(deps.values())
        for b in writes:
            b.writer = o
            b.readers = {}
        for b in reads:
            b.readers[eng if dkey is None else ("dma", dkey)] = o
        self.q[eng].append(o)
        return o

    def emit(self, final_waits=()):
        nc = self.nc
        for e in ENGS:
            c = 0
            for o in self.q[e]:
                if o.dkey is None and o.sig:
                    c += 1
                    o.idx = c
        with ExitStack() as es:
            semh = {}
            for e in ENGS:
                semh[e] = es.enter_context(nc.semaphore("s_" + e))
            for k in self.dma_cnt:
                semh[("dma", k)] = es.enter_context(nc.semaphore("d_" + str(k)))
            block = es.enter_context(nc.Block())

            def token(p):
                if p.dkey is not None:
                    return ("dma", p.dkey), p.dval
                return p.eng, p.idx

            def body(ename):
                def f(eng):
                    seen = {}
                    for o in self.q[ename]:
                        need = {}
                        for d in o.deps:
                            s, v = token(d)
                            if seen.get(s, 0) < v and need.get(s, 0) < v:
                                need[s] = v
                        for s, v in need.items():
                            eng.wait_ge(semh[s], v)
                            seen[s] = v
                        inst = o.fn(eng)
                        if o.dkey is not None:
                            inst.then_inc(semh[("dma", o.dkey)], 16)
                        elif o.sig:
                            inst.then_inc(semh[ename], 1)
                    if ename == "sp":
                        for p in final_waits:
                            s, v = token(p)
                            eng.wait_ge(semh[s], v)
                return f

            block.tensor(body("pe"))
            block.scalar(body("act"))
            block.vector(body("dve"))
            block.gpsimd(body("pool"))
            block.sync(body("sp"))


def _pp_map():
    m = {}
    c = 0

    def put(name, n):
        nonlocal c
        m[name] = c
        c += n
    put("ln_in_g", 16)
    put("ln_in_b", 16)
    for l in range(DEPTH):
        put(("gate_bias", l), 48)
        put(("conv_a_w", l), 24)
        put(("sg_ln_g", l), 8)
        put(("sg_ln_b", l), 8)
        put(("cc_conv_w", l), 248)
        put(("cc_conv_b", l), 8)
        put(("cc_ln_g", l), 8)
        put(("cc_ln_b", l), 8)
        put(("ln_mix_g", l), 16)
        put(("ln_mix_b", l), 16)
        put(("ln_ffn_g", l), 16)
        put(("ln_ffn_b", l), 16)
    return m, c


PPM, NPP = _pp_map()


class KB:
    def __init__(self, depth=DEPTH, groups=(0, 1)):
        self.depth = depth
        self.groups = groups
        nc = self.nc = bass.Bass("TRN2", target_bir_lowering=False)
        self.P = Prog(nc)
        dt = nc.dram_tensor
        self.d_x = dt("xT", [16, 128, TL], F32, kind="ExternalInput").ap()
        self.d_tmask = dt("tmask", [128, HALO], F32, kind="ExternalInput").ap()
        self.d_win = dt("win", [DEPTH, 104, 128, 2048], F32, kind="ExternalInput").ap()
        self.d_wv = dt("wvv", [DEPTH, 8, 128, 2048], F32, kind="ExternalInput").ap()
        self.d_wbr = dt("wbr", [DEPTH, 48, 128, 1024], F32, kind="ExternalInput").ap()
        self.d_wout = dt("wout", [DEPTH, 16, 128, 2048], F32, kind="ExternalInput").ap()
        self.d_wfi = dt("wfi", [DEPTH, 88, 128, 2048], F32, kind="ExternalInput").ap()
        self.d_wfo = dt("wfo", [DEPTH, 16, 128, 5632], F32, kind="ExternalInput").ap()
        self.d_pp = dt("pp", [128, NPP], F32, kind="ExternalInput").ap()
        self.d_sgw = dt("sgwT", [DEPTH, 128, 1024], F32, kind="ExternalInput").ap()
        self.d_sgb = dt("sgb", [DEPTH, 1024], F32, kind="ExternalInput").ap()
        self.d_tri = dt("tri", [128, 256], F32, kind="ExternalInput").ap()
        self.d_out = dt("out", [16, 128, OWN], F32, kind="ExternalOutput").ap()

    def sb(self, name, shape, dtype):
        return self.es.enter_context(self.nc.sbuf_tensor("sb_" + name, shape, dtype))

    def scr(self):
        i = self.scr_i
        self.scr_i = (i + 1) % NSCR
        return self.scrt[i], self.B_scr[i]

    def newps(self):
        i = self.ps_i
        self.ps_i = (i + 1) % 4
        return self.pst[i], self.B_ps[i]

    def mkviews(self):
        c0, nb, n = self.c0, self.nb, self.n

        def sv(ap2d):
            if nb == 2:
                return ap2d[:, c0:GT].rearrange("p (b n) -> p b n", b=2)
            return ap2d[:, c0:GT]

        def pv(D):
            if nb == 2:
                return D[:, :, 0:n]
            return D[:, 0, 0:n]
        return sv, pv

    def ppc(self, key, col=0):
        c = PPM[key] + col
        return self.pp[:, c:c + 1]

    def bslot(self, s):
        return self.big[:, s * GT:(s + 1) * GT]

    def y_ap(self, br, j):
        return self.bslot(br * 8 + j), [self.B_big[br * 8 + j]]

    def merged_ap(self, j):
        return self.bslot(24 + j), [self.B_big[24 + j]]

    def co_ap(self, j):
        s = 24 + 2 * j
        return self.big[:, s * GT:(s + 2) * GT].bitcast(F32), [self.B_big[s], self.B_big[s + 1]]

    def vn_ap(self, c):
        col = 36 * GT + c * 1024
        s0 = col // GT
        s1 = (col + 1023) // GT
        return self.big[:, col:col + 1024], [self.B_big[s] for s in range(s0, s1 + 1)]

    def h_ap(self, i):
        return self.bslot(i), [self.B_big[i]]

    def wseq_build(self):
        seq = []
        for g in self.groups:
            for l in range(self.depth):
                chunks = list(range((S_V[l] if g == 0 else 0) // 128, GT // 128))
                passes = [chunks[:3], chunks[3:]] if len(chunks) > 3 else [chunks]
                for j in range(8):
                    for e in (40 + j, 48 + j):
                        seq.append((("in", g, l, e), self.d_win[l, e], 2048))
                for pi, pch in enumerate(passes):
                    for u in range(8):
                        seq.append((("v", g, l, pi, u), self.d_wv[l, u], 2048))
                for j in range(8):
                    for e in (8 + j, 16 + j, j):
                        seq.append((("in", g, l, e), self.d_win[l, e], 2048))
                for gg in range(8):
                    seq.append((("in", g, l, 24 + gg), self.d_win[l, 24 + gg], 2048))
                for j in range(16):
                    for n in range(3):
                        e = 56 + n * 16 + j
                        seq.append((("in", g, l, e), self.d_win[l, e], 2048))
                        seq.append((("br", g, l, n, j), self.d_wbr[l, n * 16 + j], 1024))
                for j in range(16):
                    seq.append((("out", g, l, j), self.d_wout[l, j], 2048))
                for i in range(44):
                    seq.append((("fi", g, l, 2 * i), self.d_wfi[l, 2 * i], 2048))
                    seq.append((("fi", g, l, 2 * i + 1), self.d_wfi[l, 2 * i + 1], 2048))
                for j in range(16):
                    for (k0, nk) in ((0, 16), (16, 16), (32, 12)):
                        seq.append((("fo", g, l, j, k0), self.d_wfo[l, j][:, k0 * 128:(k0 + nk) * 128], nk * 128))
        self.wseq = seq
        self.w_issued = 0
        self.w_pos = 0

    def wget(self, key, hold=False):
        seq = self.wseq
        assert seq[self.w_pos][0] == key, (seq[self.w_pos][0], key)
        while self.w_issued < min(len(seq), self.w_pos + (1 if hold else NS)):
            i = self.w_issued
            _, src, ncols = seq[i]
            s = i % NS
            self.P.op("pool", lambda e, s=s, src=src, ncols=ncols: e.dma_start(out=self.ring[s][:, 0:ncols], in_=src),
                      writes=[self.B_ring[s]], dkey="w%d" % s)
            self.w_issued += 1
        s = self.w_pos % NS
        self.w_pos += 1
        return self.ring[s], self.B_ring[s]

    def mm_fm(self, D, BD, wt, Bw, nk, act3, Bact, kbase=0, start=True, stop=True):
        c0, n, nb = self.c0, self.n, self.nb

        def f(e):
            r = None
            for b in range(nb):
                for k in range(nk):
                    r = e.matmul(D[:, b, 0:n], lhsT=wt[:, k * 128:(k + 1) * 128],
                                 rhs=act3[:, kbase + k, c0 + b * n:c0 + (b + 1) * n],
                                 start=(start and k == 0), stop=(stop and k == nk - 1))
            return r
        self.P.op("pe", f, reads=[Bw] + list(Bact), writes=[BD])

    def kouter(self, keys):
        P = self.P
        c0, n, nb = self.c0, self.n, self.nb
        xh3 = self.xh[:, :].rearrange("p (j t) -> p j t", j=16)
        units = []
        for i, key in enumerate(keys):
            wt, Bw = self.wget(key, hold=(i > 0))
            D, BD = self.newps()
            units.append((key, wt, Bw, D, BD))
        for k in range(16):
            def f(e, k=k):
                r = None
                for (_, wt, _, D, _) in units:
                    for b in range(nb):
                        r = e.matmul(D[:, b, 0:n], lhsT=wt[:, k * 128:(k + 1) * 128],
                                     rhs=xh3[:, k, c0 + b * n:c0 + (b + 1) * n], start=(k == 0), stop=(k == 15))
                return r
            P.op("pe", f, reads=[self.B_xh[k]] + [u[2] for u in units], writes=[u[4] for u in units])
        for u in units:
            self.pre[u[0]] = (u[3], u[4])

    def unit_xh(self, key):
        if key in self.pre:
            return self.pre.pop(key)
        wt, Bw = self.wget(key)
        D, BD = self.newps()
        xh3 = self.xh[:, :].rearrange("p (j t) -> p j t", j=16)
        self.mm_fm(D, BD, wt, Bw, 16, xh3, self.B_xh)
        return D, BD

    def st_begin(self):
        self.st_n = 0

    def st_add(self, tap, tb):
        P = self.P
        c0 = self.c0
        accs, Bas = self.lnt[0], self.B_lnt[0]
        accq, Baq = self.lnt[1], self.B_lnt[1]
        j = self.st_n
        self.st_n += 1
        if j == 0:
            self.st_first = (tap, tb)
            P.op("act", lambda e: e.activation(out=accq[:, c0:GT], in_=tap[:, c0:GT], func=AF.Square),
                 reads=list(tb), writes=[Baq])
            return
        if j == 1:
            t0ap, t0b = self.st_first
            P.op("dve", lambda e: e.tensor_tensor(out=accs[:, c0:GT], in0=t0ap[:, c0:GT], in1=tap[:, c0:GT], op=ALU.add),
                 reads=list(tb) + list(t0b), writes=[Bas])
        else:
            P.op("dve", lambda e: e.tensor_tensor(out=accs[:, c0:GT], in0=accs[:, c0:GT], in1=tap[:, c0:GT], op=ALU.add),
                 reads=list(tb) + [Bas], writes=[Bas])
        sq, Bsq = self.scr()
        P.op("act", lambda e: e.activation(out=sq[:, c0:GT], in_=tap[:, c0:GT], func=AF.Square),
             reads=list(tb), writes=[Bsq])
        P.op("dve", lambda e: e.tensor_tensor(out=accq[:, c0:GT], in0=accq[:, c0:GT], in1=sq[:, c0:GT], op=ALU.add),
             reads=[Bsq, Baq], writes=[Baq])

    def st_finish(self, nfeat):
        P = self.P
        c0, n, nb = self.c0, self.n, self.nb
        sv, pv = self.mkviews()
        accs, Bas = self.lnt[0], self.B_lnt[0]
        accq, Baq = self.lnt[1], self.B_lnt[1]
        DS, BS = self.newps()
        DQ, BQ = self.newps()

        def fs(e):
            r = None
            for b in range(nb):
                r = e.matmul(DS[:, b, 0:n], lhsT=self.ones32[:, :], rhs=accs[:, c0 + b * n:c0 + (b + 1) * n], start=True, stop=True)
            return r
        P.op("pe", fs, reads=[Bas, self.B_const], writes=[BS])

        def fq(e):
            r = None
            for b in range(nb):
                r = e.matmul(DQ[:, b, 0:n], lhsT=self.ones32[:, :], rhs=accq[:, c0 + b * n:c0 + (b + 1) * n], start=True, stop=True)
            return r
        P.op("pe", fq, reads=[Baq, self.B_const], writes=[BQ])
        inv = 1.0 / nfeat
        mean, Bm = self.lnt[0], self.B_lnt[0]
        rstd, Br = self.lnt[1], self.B_lnt[1]
        t1, Bt1 = self.lnt[2], self.B_lnt[2]
        P.op("dve", lambda e: e.tensor_scalar(out=sv(mean), in0=pv(DS), scalar1=inv, scalar2=None, op0=ALU.mult),
             reads=[BS], writes=[Bm])
        P.op("dve", lambda e: e.tensor_tensor(out=t1[:, c0:GT], in0=mean[:, c0:GT], in1=mean[:, c0:GT], op=ALU.mult),
             reads=[Bm], writes=[Bt1])
        P.op("dve", lambda e: e.scalar_tensor_tensor(out=sv(rstd), in0=pv(DQ), scalar=inv, in1=sv(t1),
                                                     op0=ALU.mult, op1=ALU.subtract),
             reads=[BQ, Bt1], writes=[Br])
        P.op("act", lambda e: e.activation(out=t1[:, c0:GT], in_=rstd[:, c0:GT], func=AF.Sqrt, bias=self.epsc[:, 0:1], scale=1.0),
             reads=[Br, self.B_const], writes=[Bt1])

        def recip():
            P.op("dve", lambda e: e.reciprocal(out=rstd[:, c0:GT], in_=t1[:, c0:GT]),
                 reads=[Bt1], writes=[Br])
        self.ln_mean = (mean, Bm)
        return DS, BS, -inv, rstd, Br, recip

    def ln_apply(self, tiles, gkey, nfeat, outs):
        P = self.P
        c0 = self.c0
        sv, pv = self.mkviews()
        DS, BS, ninv, rstd, Br, recip = self.st_finish(nfeat)
        mean, Bm = self.ln_mean
        nt = len(tiles)
        for b0 in range(0, nt, 8):
            ds = []
            for j in range(b0, min(nt, b0 + 8)):
                tap, tb = tiles[j]
                d, Bd = self.scr()
                ds.append((j, d, Bd))
                P.op("dve", lambda e, tap=tap, d=d: e.tensor_tensor(out=d[:, c0:GT], in0=tap[:, c0:GT], in1=mean[:, c0:GT], op=ALU.subtract),
                     reads=list(tb) + [Bm], writes=[Bd])
            if b0 == 0:
                recip()
            for (j, d, Bd) in ds:
                P.op("dve", lambda e, d=d, j=j: e.scalar_tensor_tensor(out=d[:, c0:GT], in0=d[:, c0:GT], scalar=self.ppc(gkey, j),
                                                                     in1=rstd[:, c0:GT], op0=ALU.mult, op1=ALU.mult),
                     reads=[Bd, Br, self.B_pp], writes=[Bd])
                outs(j, d, Bd)

    def ln_x(self, gkey, bkey, accumulated=False, final=False, src=None):
        P = self.P
        c0 = self.c0
        sv, pv = self.mkviews()
        if src is None:
            src = [(self.x32[:, j * GT:(j + 1) * GT], [self.B_x32[j]]) for j in range(16)]
        if not accumulated:
            self.st_begin()
            for j in range(16):
                self.st_add(*src[j])
        DS, BS, ninv, rstd, Br, recip = self.st_finish(D_MODEL)
        mean, Bm = self.ln_mean
        for j in range(4, 16):
            xj = self.x32[:, j * GT:(j + 1) * GT]
            sj, sb_ = src[j]
            P.op("pool", lambda e, xj=xj, sj=sj: e.tensor_tensor(out=xj[:, c0:GT], in0=sj[:, c0:GT], in1=mean[:, c0:GT], op=ALU.subtract),
                 reads=list(sb_) + [Bm], writes=[self.B_x32[j]])
        for j in range(4):
            xj = self.x32[:, j * GT:(j + 1) * GT]
            sj, sb_ = src[j]
            P.op("dve", lambda e, xj=xj, sj=sj: e.scalar_tensor_tensor(out=sv(xj), in0=pv(DS), scalar=ninv, in1=sv(sj),
                                                                      op0=ALU.mult, op1=ALU.add),
                 reads=list(sb_) + [BS], writes=[self.B_x32[j]])
            if j == 3:
                recip()
        for j in range(16):
            xj = self.x32[:, j * GT:(j + 1) * GT]
            hj = self.xh[:, j * GT:(j + 1) * GT]
            P.op("dve", lambda e, xj=xj, j=j: e.scalar_tensor_tensor(out=xj[:, c0:GT], in0=xj[:, c0:GT], scalar=self.ppc(gkey, j),
                                                                   in1=rstd[:, c0:GT], op0=ALU.mult, op1=ALU.mult),
                 reads=[self.B_x32[j], Br, self.B_pp], writes=[self.B_x32[j]])
            if final:
                P.op("act", lambda e, xj=xj, j=j: e.activation(out=xj[:, c0:GT], in_=xj[:, c0:GT], func=AF.Identity,
                                                             bias=self.ppc(bkey, j), scale=1.0),
                     reads=[self.B_x32[j], self.B_pp], writes=[self.B_x32[j]])
                continue
            P.op("act", lambda e, xj=xj, hj=hj, j=j: e.activation(out=hj[:, c0:GT], in_=xj[:, c0:GT], func=AF.Identity,
                                                                 bias=self.ppc(bkey, j), scale=1.0),
                 reads=[self.B_x32[j], self.B_pp], writes=[self.B_xh[j]])

            def late(xj=xj, j=j):
                P.op("act", lambda e: e.activation(out=xj[:, c0:GT], in_=xj[:, c0:GT], func=AF.Identity,
                                                   bias=self.ppc(bkey, j), scale=1.0),
                     reads=[self.B_x32[j], self.B_pp], writes=[self.B_x32[j]])
            self.deferred.append(late)

    def run_deferred(self, k=1):
        while k > 0 and self.deferred:
            self.deferred.pop(0)()
            k -= 1

    def flush_deferred(self):
        self.run_deferred(len(self.deferred))

    def layer(self, g, l):
        P = self.P
        c0 = self.c0 = (S_L[l] if g == 0 else 0)
        Tg = GT - c0
        self.nb = nb = 2 if Tg > 512 else 1
        self.n = n = Tg // nb
        sv, pv = self.mkviews()
        xh3 = self.xh[:, :].rearrange("p (j t) -> p j t", j=16)
        big3 = self.big[:, :].rearrange("p (j t) -> p j t", j=44)
        nmask = max(0, HALO - c0) if g == 0 else 0

        P.op("pool", lambda e: e.dma_start(out=self.wTm[:, :], in_=self.d_sgw[l]), writes=[self.B_wTm], dkey="sgw")
        P.op("dve", lambda e: e.tensor_tensor(out=self.wTm[:, :].rearrange("p (g t) -> p g t", g=8),
                                              in0=self.wTm[:, :].rearrange("p (g t) -> p g t", g=8),
                                              in1=self.tri[:, 0:128].unsqueeze(1).to_broadcast([128, 8, 128]), op=ALU.mult),
             reads=[self.B_wTm, self.B_const], writes=[self.B_wTm])
        P.op("sp", lambda e: e.dma_start(out=self.Cc[:, :], in_=self.d_sgb[l].partition_broadcast(128)),
             writes=[self.B_Cc], dkey="sgb")
        def c_stage1(j):
            DA, BA = self.unit_xh(("in", g, l, 40 + j))
            DG, BG = self.unit_xh(("in", g, l, 48 + j))
            s1, Bs1 = self.scr()
            P.op("act", lambda e, s1=s1, DG=DG: e.activation(out=sv(s1), in_=pv(DG), func=AF.Sigmoid), reads=[BG], writes=[Bs1])
            self.run_deferred(2)
            ci = self.cv_i
            self.cv_i = (ci + 1) % 2
            yp, Byp = self.ygb[ci], self.B_ygb[ci]
            dg, Bdg = self.diag[ci], self.B_diag[ci]
            yv = yp[:, 30:30 + GT]
            tb = (l * 8 + j) * 30
            wc = PPM[("cc_conv_w", l)] + j * 31
            P.op("dve", lambda e, dg=dg, wc=wc: e.tensor_tensor(
                out=dg[:, :].rearrange("p (k m) -> p k m", k=31),
                in0=self.tri[:, 128:256].unsqueeze(1).to_broadcast([128, 31, 128]),
                in1=self.pp[:, wc:wc + 31].unsqueeze(2).to_broadcast([128, 31, 128]), op=ALU.mult),
                reads=[self.B_const, self.B_pp], writes=[Bdg])
            if g == 0:
                P.op("dve", lambda e, yp=yp: e.memset(yp[:, c0:c0 + 30], 0.0), writes=[Byp])
            else:
                P.op("dve", lambda e, yp=yp, tb=tb: e.tensor_copy(out=yp[:, 0:30], in_=self.ctail[:, tb:tb + 30]),
                     reads=[self.B_ctail[l * 8 + j]], writes=[Byp])
            P.op("dve", lambda e, yv=yv, s1=s1, DA=DA: e.tensor_tensor(out=sv(yv), in0=sv(s1), in1=pv(DA), op=ALU.mult),
                 reads=[Bs1, BA], writes=[Byp])
            if nmask:
                P.op("dve", lambda e, yv=yv: e.tensor_tensor(out=yv[:, c0:HALO], in0=yv[:, c0:HALO], in1=self.tmask[:, c0:HALO], op=ALU.mult),
                     reads=[Byp, self.B_const], writes=[Byp])
            if g == 0:
                P.op("dve", lambda e, yp=yp, tb=tb: e.tensor_copy(out=self.ctail[:, tb:tb + 30], in_=yp[:, GT:GT + 30]),
                     reads=[Byp], writes=[self.B_ctail[l * 8 + j]])
            return (j, yp, Byp, dg, Bdg)

        def c_stage2(ctx):
            j, yp, Byp, dg, Bdg = ctx
            DV, BV = self.newps()

            def fcv(e, DV=DV, dg=dg, yp=yp):
                r = None
                for b in range(nb):
                    for k in range(31):
                        r = e.matmul(DV[:, b, 0:n], lhsT=dg[:, k * 128:(k + 1) * 128],
                                     rhs=yp[:, c0 + b * n + k:c0 + b * n + k + n], start=(k == 0), stop=(k == 30))
                return r
            P.op("pe", fcv, reads=[Bdg, Byp], writes=[BV])
            co, Bco = self.co_ap(j)
            P.op("act", lambda e, DV=DV, co=co, j=j: e.activation(out=sv(co), in_=pv(DV), func=AF.Identity,
                                                                 bias=self.ppc(("cc_conv_b", l), j), scale=1.0),
                 reads=[BV, self.B_pp], writes=Bco)
            if j == 0:
                self.st_begin()
            self.st_add(co, Bco)

        self.kouter([("in", g, l, 40), ("in", g, l, 48), ("in", g, l, 41)])
        pend = None
        for j in range(8):
            ctx = c_stage1(j)
            if pend is not None:
                c_stage2(pend)
            pend = ctx
        c_stage2(pend)
        DR, BR = self.newps()

        def frs(e):
            r = None
            for b in range(2):
                r = e.matmul(DR[:, b, :], lhsT=self.ones16[:, :], rhs=self.wTm[:, b * 512:(b + 1) * 512], start=True, stop=True)
            return r
        P.op("pe", frs, reads=[self.B_wTm, self.B_const], writes=[BR])
        for gg in range(8):
            P.op("dve", lambda e, gg=gg: e.scalar_tensor_tensor(
                out=self.Cc[:, gg * 128:(gg + 1) * 128], in0=DR[:, gg // 4, (gg % 4) * 128:(gg % 4 + 1) * 128],
                scalar=self.ppc(("sg_ln_b", l), gg), in1=self.Cc[:, gg * 128:(gg + 1) * 128], op0=ALU.mult, op1=ALU.add),
                reads=[BR, self.B_pp, self.B_Cc], writes=[self.B_Cc])

        tiles = [self.co_ap(j) for j in range(8)]

        def outs_c(j, d, Bd):
            yc, Byc = self.y_ap(2, j)
            P.op("act", lambda e: e.activation(out=yc[:, c0:GT], in_=d[:, c0:GT], func=AF.Silu,
                                               bias=self.ppc(("cc_ln_b", l), j), scale=1.0),
                 reads=[Bd, self.B_pp], writes=Byc)
        self.ln_apply(tiles, ("cc_ln_g", l), 1024, outs_c)

        chunks = list(range((S_V[l] if g == 0 else 0) // 128, GT // 128))
        passes = [chunks[:3], chunks[3:]] if len(chunks) > 3 else [chunks]
        for pi, pch in enumerate(passes):
            Ds = [self.newps() for _ in pch]
            for u in range(8):
                wt, Bw = self.wget(("v", g, l, pi, u))
                for ci, c in enumerate(pch):
                    D, BD = Ds[ci]

                    def fv(e, wt=wt, D=D, c=c, u=u):
                        r = None
                        for kk in range(2):
                            k = 2 * u + kk
                            for cb in range(2):
                                r = e.matmul(D[:, cb, :], lhsT=xh3[:, k, c * 128:(c + 1) * 128],
                                             rhs=wt[:, kk * 1024 + cb * 512:kk * 1024 + (cb + 1) * 512],
                                             start=(k == 0), stop=(k == 15))
                        return r
                    P.op("pe", fv, reads=[Bw, self.B_xh[2 * u], self.B_xh[2 * u + 1]], writes=[BD])
            for ci, c in enumerate(pch):
                D, BD = Ds[ci]
                si = self.vs_i
                self.vs_i = (si + 1) % 2
                st = self.vst[si]
                Bst = self.B_vst[si]
                P.op("dve", lambda e, D=D, st=st: e.bn_stats(out=st[:, 0:6], in_=D[:, 0, :]), reads=[BD], writes=[Bst])
                P.op("dve", lambda e, D=D, st=st: e.bn_stats(out=st[:, 6:12], in_=D[:, 1, :]), reads=[BD], writes=[Bst])
                P.op("dve", lambda e, st=st: e.bn_aggr(out=st[:, 12:14], in_=st[:, 0:12]), reads=[Bst], writes=[Bst])
                P.op("dve", lambda e, st=st: e.tensor_scalar(out=st[:, 14:15], in0=st[:, 13:14], scalar1=LN_EPS, scalar2=None, op0=ALU.add),
                     reads=[Bst], writes=[Bst])
                P.op("act", lambda e, st=st: e.activation(out=st[:, 15:16], in_=st[:, 14:15], func=AF.Sqrt), reads=[Bst], writes=[Bst])
                P.op("dve", lambda e, st=st: e.reciprocal(out=st[:, 16:17], in_=st[:, 15:16]), reads=[Bst], writes=[Bst])
                vn, Bvn = self.vn_ap(c)
                P.op("dve", lambda e, D=D, st=st, vn=vn: e.tensor_scalar(
                    out=vn.rearrange("p (b n) -> p b n", b=2), in0=D[:, :, :], scalar1=st[:, 12:13], scalar2=st[:, 16:17],
                    op0=ALU.subtract, op1=ALU.mult), reads=[BD, Bst], writes=Bvn)

        for j in range(8):
            wt, Bw = self.wget(("in", g, l, 8 + j))
            DC, BC = self.newps()
            self.mm_fm(DC, BC, wt, Bw, 16, xh3, self.B_xh)
            wt, Bw = self.wget(("in", g, l, 16 + j))
            DH, BH = self.newps()
            self.mm_fm(DH, BH, wt, Bw, 16, xh3, self.B_xh)
            wt, Bw = self.wget(("in", g, l, j))
            DB, BB = self.newps()
            self.mm_fm(DB, BB, wt, Bw, 16, xh3, self.B_xh)
            s1, Bs1 = self.scr()
            P.op("act", lambda e, s1=s1, DC=DC: e.activation(out=sv(s1), in_=pv(DC), func=AF.Copy), reads=[BC], writes=[Bs1])
            pp_, Bpp_ = self.scr()
            pv2 = pp_[:, 2:2 + GT]
            if g == 0:
                P.op("dve", lambda e, pp_=pp_: e.memset(pp_[:, c0:c0 + 2], 0.0), writes=[Bpp_])
            else:
                P.op("dve", lambda e, pp_=pp_, j=j: e.tensor_copy(out=pp_[:, 0:2], in_=self.atail[:, (l * 8 + j) * 2:(l * 8 + j) * 2 + 2]),
                     reads=[self.B_atail[l * 8 + j]], writes=[Bpp_])
            P.op("dve", lambda e, pv2=pv2, s1=s1, DH=DH: e.tensor_tensor(out=sv(pv2), in0=sv(s1), in1=pv(DH), op=ALU.mult),
                 reads=[Bs1, BH], writes=[Bpp_])
            if nmask:
                P.op("dve", lambda e, pv2=pv2: e.tensor_tensor(out=pv2[:, c0:HALO], in0=pv2[:, c0:HALO], in1=self.tmask[:, c0:HALO], op=ALU.mult),
                     reads=[Bpp_, self.B_const], writes=[Bpp_])
            if g == 0:
                P.op("dve", lambda e, pp_=pp_, j=j: e.tensor_copy(out=self.atail[:, (l * 8 + j) * 2:(l * 8 + j) * 2 + 2], in_=pp_[:, GT:GT + 2]),
                     reads=[Bpp_], writes=[self.B_atail[l * 8 + j]])
            acc, Bacc = self.scr()
            wc = PPM[("conv_a_w", l)] + j * 3
            P.op("dve", lambda e, acc=acc, pp_=pp_, wc=wc: e.tensor_scalar(out=acc[:, c0:GT], in0=pp_[:, c0:GT], scalar1=self.pp[:, wc:wc + 1],
                                                                          scalar2=None, op0=ALU.mult),
                 reads=[Bpp_, self.B_pp], writes=[Bacc])
            for k in (1, 2):
                P.op("dve", lambda e, acc=acc, pp_=pp_, wc=wc, k=k: e.scalar_tensor_tensor(
                    out=acc[:, c0:GT], in0=pp_[:, c0 + k:GT + k], scalar=self.pp[:, wc + k:wc + k + 1], in1=acc[:, c0:GT],
                    op0=ALU.mult, op1=ALU.add), reads=[Bpp_, Bacc, self.B_pp], writes=[Bacc])
            ya, Bya = self.y_ap(0, j)
            P.op("dve", lambda e, acc=acc, DB=DB, ya=ya: e.tensor_tensor(out=sv(ya), in0=sv(acc), in1=pv(DB), op=ALU.mult),
                 reads=[Bacc, BB], writes=Bya)

        for gg in range(8):
            wt, Bw = self.wget(("in", g, l, 24 + gg))
            DU, BU = self.newps()
            self.mm_fm(DU, BU, wt, Bw, 16, xh3, self.B_xh)
            DM, BM = self.newps()

            def fm(e, gg=gg, DM=DM):
                r = None
                for c in chunks:
                    a = c * 128 - c0
                    bnd = [max(a, 0), a + 128]
                    if nb == 2 and bnd[0] < n < a + 128:
                        bnd = [bnd[0], n, a + 128]
                    for i in range(len(bnd) - 1):
                        lo, hi = bnd[i], bnd[i + 1]
                        b = lo // n
                        vn, _ = self.vn_ap(c)
                        r = e.matmul(DM[:, b, lo - b * n:hi - b * n], lhsT=vn[:, gg * 128:(gg + 1) * 128],
                                     rhs=self.wTm[:, gg * 128 + (lo - a):gg * 128 + (hi - a)], start=True, stop=True,
                                     skip_group_check=True)
                return r
            rd = [self.B_wTm]
            for c in chunks:
                rd += self.vn_ap(c)[1]
            P.op("pe", fm, reads=rd, writes=[BM])
            t, Bt = self.scr()
            nch = len(chunks)
            if nb == 1 or True:
                for b in range(nb):
                    lo = c0 + b * n
                    P.op("dve", lambda e, gg=gg, DM=DM, t=t, b=b, lo=lo: e.tensor_scalar(
                        out=t[:, lo:lo + n], in0=DM[:, b, 0:n], scalar1=self.ppc(("sg_ln_g", l), gg), scalar2=None, op0=ALU.mult),
                        reads=[BM, self.B_pp], writes=[Bt])
            cs0 = ((c0 + 127) // 128) * 128
            nch = (GT - cs0) // 128
            if cs0 > c0 and chunks[0] * 128 < cs0:
                P.op("dve", lambda e, gg=gg, t=t, cs0=cs0: e.tensor_tensor(
                    out=t[:, c0:cs0], in0=t[:, c0:cs0],
                    in1=self.Cc[:, gg * 128 + 128 - (cs0 - c0):(gg + 1) * 128], op=ALU.add),
                    reads=[Bt, self.B_Cc], writes=[Bt])
            P.op("dve", lambda e, gg=gg, t=t, cs0=cs0: e.tensor_tensor(
                out=t[:, cs0:GT].rearrange("p (c t) -> p c t", t=128), in0=t[:, cs0:GT].rearrange("p (c t) -> p c t", t=128),
                in1=self.Cc[:, gg * 128:(gg + 1) * 128].unsqueeze(1).to_broadcast([128, nch, 128]), op=ALU.add),
                reads=[Bt, self.B_Cc], writes=[Bt])
            yb, Byb = self.y_ap(1, gg)
            P.op("dve", lambda e, t=t, DU=DU, yb=yb: e.tensor_tensor(out=sv(yb), in0=sv(t), in1=pv(DU), op=ALU.mult),
                 reads=[Bt, BU], writes=Byb)

        for j in range(16):
            m, Bm_ = self.scr()
            mg, Bmg = self.merged_ap(j)
            for nbr in range(3):
                wt, Bw = self.wget(("in", g, l, 56 + nbr * 16 + j))
                DG, BG = self.newps()
                self.mm_fm(DG, BG, wt, Bw, 16, xh3, self.B_xh)
                wt, Bw = self.wget(("br", g, l, nbr, j))
                DP, BP = self.newps()
                self.mm_fm(DP, BP, wt, Bw, 8, big3, [self.B_big[nbr * 8 + k] for k in range(8)], kbase=nbr * 8)
                gs, Bgs = self.scr()
                P.op("act", lambda e, gs=gs, DG=DG, nbr=nbr, j=j: e.activation(out=sv(gs), in_=pv(DG), func=AF.Sigmoid,
                                                                             bias=self.ppc(("gate_bias", l), nbr * 16 + j), scale=1.0),
                     reads=[BG, self.B_pp], writes=[Bgs])
                if nbr == 0:
                    P.op("dve", lambda e, m=m, gs=gs, DP=DP: e.tensor_tensor(out=sv(m), in0=sv(gs), in1=pv(DP), op=ALU.mult),
                         reads=[Bgs, BP], writes=[Bm_])
                else:
                    P.op("dve", lambda e, gs=gs, DP=DP: e.tensor_tensor(out=sv(gs), in0=sv(gs), in1=pv(DP), op=ALU.mult),
                         reads=[Bgs, BP], writes=[Bgs])
                    if nbr == 1:
                        P.op("dve", lambda e, m=m, gs=gs: e.tensor_tensor(out=m[:, c0:GT], in0=m[:, c0:GT], in1=gs[:, c0:GT], op=ALU.add),
                             reads=[Bm_, Bgs], writes=[Bm_])
                    else:
                        P.op("dve", lambda e, m=m, gs=gs, mg=mg: e.tensor_tensor(out=mg[:, c0:GT], in0=m[:, c0:GT], in1=gs[:, c0:GT], op=ALU.add),
                             reads=[Bm_, Bgs], writes=Bmg)

        self.flush_deferred()
        for j in range(16):
            wt, Bw = self.wget(("out", g, l, j))
            DO, BO = self.newps()
            self.mm_fm(DO, BO, wt, Bw, 16, big3, [self.B_big[24 + k] for k in range(16)], kbase=24)
            xj = self.x32[:, j * GT:(j + 1) * GT]
            P.op("dve", lambda e, xj=xj, DO=DO: e.scalar_tensor_tensor(out=sv(xj), in0=sv(xj), scalar=ALPHA, in1=pv(DO),
                                                                      op0=ALU.mult, op1=ALU.add),
                 reads=[self.B_x32[j], BO], writes=[self.B_x32[j]])
            if j == 0:
                self.st_begin()
            self.st_add(xj, [self.B_x32[j]])
        self.ln_x(("ln_mix_g", l), ("ln_mix_b", l), accumulated=True)

        self.kouter([("fi", g, l, 0), ("fi", g, l, 1), ("fi", g, l, 2)])
        for i in range(44):
            DG, BG = self.unit_xh(("fi", g, l, 2 * i))
            DU, BU = self.unit_xh(("fi", g, l, 2 * i + 1))
            s1, Bs1 = self.scr()
            P.op("act", lambda e, s1=s1, DG=DG: e.activation(out=sv(s1), in_=pv(DG), func=AF.Silu), reads=[BG], writes=[Bs1])
            self.run_deferred(1)
            hi, Bhi = self.h_ap(i)
            P.op("dve", lambda e, s1=s1, DU=DU, hi=hi: e.tensor_tensor(out=sv(hi), in0=sv(s1), in1=pv(DU), op=ALU.mult),
                 reads=[Bs1, BU], writes=Bhi)
        self.flush_deferred()
        for j in range(16):
            DF, BF = self.newps()
            for (k0, nk) in ((0, 16), (16, 16), (32, 12)):
                wt, Bw = self.wget(("fo", g, l, j, k0))
                self.mm_fm(DF, BF, wt, Bw, nk, big3, [self.B_big[k0 + k] for k in range(nk)], kbase=k0,
                           start=(k0 == 0), stop=(k0 == 32))
            xj = self.x32[:, j * GT:(j + 1) * GT]
            P.op("dve", lambda e, xj=xj, DF=DF: e.scalar_tensor_tensor(out=sv(xj), in0=sv(xj), scalar=ALPHA, in1=pv(DF),
                                                                      op0=ALU.mult, op1=ALU.add),
                 reads=[self.B_x32[j], BF], writes=[self.B_x32[j]])
            if j == 0:
                self.st_begin()
            self.st_add(xj, [self.B_x32[j]])
        if l == self.depth - 1 and self.prefetch_next is not None:
            self.prefetch_next()
            self.prefetch_next = None
        self.ln_x(("ln_ffn_g", l), ("ln_ffn_b", l), accumulated=True, final=(l == self.depth - 1))

    def build(self):
        nc = self.nc
        P = self.P
        with ExitStack() as es:
            self.es = es
            self.x32 = self.sb("x32", [128, 16 * GT], F32)
            self.xh = self.sb("xh", [128, 16 * GT], BF16)
            self.big = self.sb("big", [128, 44 * GT], BF16)
            self.ring = [self.sb("ring%d" % i, [128, 2048], BF16) for i in range(NS)]
            self.scrt = [self.sb("scr%d" % i, [128, 704], F32) for i in range(NSCR)]
            self.ygb = [self.sb("ygb%d" % i, [128, 30 + GT + 2], BF16) for i in range(2)]
            self.diag = [self.sb("diag%d" % i, [128, 31 * 128], BF16) for i in range(2)]
            self.B_ygb = [Buf("ygb%d" % i) for i in range(2)]
            self.B_diag = [Buf("diag%d" % i) for i in range(2)]
            self.cv_i = 0
            self.pp = self.sb("pp", [128, NPP], F32)
            self.ones32 = self.sb("ones32", [128, 128], F32)
            self.ones16 = self.sb("ones16", [128, 128], BF16)
            self.epsc = self.sb("epsc", [128, 2], F32)
            self.tri = self.sb("tri", [128, 256], F32)
            self.tmask = self.sb("tmask", [128, HALO], F32)
            self.wTm = self.sb("wTm", [128, 1024], BF16)
            self.Cc = self.sb("Cc", [128, 1024], F32)
            self.ctail = self.sb("ctail", [128, DEPTH * 8 * 30], BF16)
            self.atail = self.sb("atail", [128, DEPTH * 8 * 2], F32)
            self.vst = [self.sb("vst%d" % i, [128, 24], F32) for i in range(2)]
            self.lnt = [self.sb("lnt%d" % i, [128, GT], F32) for i in range(3)]
            self.B_lnt = [Buf("lnt%d" % i) for i in range(3)]
            self.pst = [es.enter_context(nc.psum_tensor("ps%d" % i, [128, 2, 512], F32)) for i in range(4)]
            self.B_x32 = [Buf("x32_%d" % j) for j in range(16)]
            self.B_xh = [Buf("xh_%d" % j) for j in range(16)]
            self.B_big = [Buf("big_%d" % j) for j in range(44)]
            self.B_ring = [Buf("ring_%d" % j) for j in range(NS)]
            self.B_scr = [Buf("scr_%d" % j) for j in range(NSCR)]
            self.B_ps = [Buf("ps_%d" % j) for j in range(4)]
            self.B_pp = Buf("pp")
            self.B_const = Buf("const")
            self.B_wTm = Buf("wTm")
            self.B_Cc = Buf("Cc")
            self.B_ctail = [Buf("ct%d" % j) for j in range(DEPTH * 8)]
            self.B_atail = [Buf("at%d" % j) for j in range(DEPTH * 8)]
            self.B_vst = [Buf("vst%d" % j) for j in range(2)]
            self.deferred = []
            self.pre = {}
            self.scr_i = 0
            self.ps_i = 0
            self.vs_i = 0
            self.wseq_build()

            P.op("sp", lambda e: e.dma_start(out=self.pp[:, :], in_=self.d_pp), writes=[self.B_pp], dkey="pp")
            P.op("sp", lambda e: e.dma_start(out=self.tri[:, :], in_=self.d_tri), writes=[self.B_const], dkey="c0")
            P.op("sp", lambda e: e.dma_start(out=self.tmask[:, :], in_=self.d_tmask), writes=[self.B_const], dkey="c0")
            P.op("dve", lambda e: e.memset(self.ones32[:, :], 1.0), writes=[self.B_const])
            P.op("dve", lambda e: e.memset(self.ones16[:, :], 1.0), writes=[self.B_const])
            P.op("dve", lambda e: e.memset(self.epsc[:, :], LN_EPS), writes=[self.B_const])

            x3 = self.x32[:, :].rearrange("p (j t) -> p j t", j=16)
            xin = self.d_x.rearrange("j p t -> p j t")
            oo = self.d_out.rearrange("j p t -> p j t")
            finals = []
            stage3 = self.big[:, 0:32 * GT].bitcast(F32).rearrange("p (j t) -> p j t", j=16)

            def load_x(g, staged):
                dst = stage3 if staged else x3
                for q in range(4):
                    wr = ([self.B_big[2 * (4 * q + i) + h] for i in range(4) for h in range(2)] if staged
                          else [self.B_x32[4 * q + i] for i in range(4)])
                    P.op("sp", lambda e, q=q: e.dma_start(out=dst[:, 4 * q:4 * q + 4, :],
                                                          in_=xin[:, 4 * q:4 * q + 4, g * GT:(g + 1) * GT]),
                         writes=wr, dkey="xin%d" % q)
            self.prefetch_next = None
            for gi, g in enumerate(self.groups):
                staged = gi > 0
                if not staged:
                    load_x(g, False)
                if gi + 1 < len(self.groups):
                    gn = self.groups[gi + 1]
                    self.prefetch_next = lambda gn=gn: load_x(gn, True)
                self.c0 = 0
                self.nb = 2
                self.n = GT // 2
                if staged:
                    src = [(self.big[:, 2 * j * GT:(2 * j + 2) * GT].bitcast(F32), [self.B_big[2 * j], self.B_big[2 * j + 1]])
                           for j in range(16)]
                    self.ln_x("ln_in_g", "ln_in_b", src=src)
                else:
                    self.ln_x("ln_in_g", "ln_in_b")
                for l in range(self.depth):
                    self.layer(g, l)
                self.flush_deferred()
                if g == 0:
                    o = P.op("sp", lambda e: e.dma_start(out=oo[:, :, 0:GT - HALO], in_=x3[:, :, HALO:GT]),
                             reads=self.B_x32, dkey="out0")
                else:
                    o = P.op("sp", lambda e: e.dma_start(out=oo[:, :, GT - HALO:OWN], in_=x3[:, :, 0:GT]),
                             reads=self.B_x32, dkey="out1")
                finals.append(o)
            assert self.w_pos == len(self.wseq), (self.w_pos, len(self.wseq))
            P.emit(final_waits=finals)
        return nc


def _fm(v, nt):
    return np.ascontiguousarray(np.asarray(v, np.float32).reshape(nt, 128).T)


def _prep_shared(inp):
    f32 = np.float32
    w_in = np.asarray(inp["w_in"], f32)
    L = DEPTH
    win = np.ascontiguousarray(w_in.reshape(L, 16, 128, 104, 128).transpose(0, 3, 2, 1, 4)).reshape(L, 104, 128, 2048)
    wv = w_in[:, :, 4096:5120]
    wvv = np.ascontiguousarray(wv.reshape(L, 8, 2, 128, 1024).transpose(0, 1, 3, 2, 4)).reshape(L, 8, 128, 2048)
    w_br = np.asarray(inp["w_branch"], f32)
    wbr = np.ascontiguousarray(w_br.reshape(L, 3, 8, 128, 16, 128).transpose(0, 1, 4, 3, 2, 5)).reshape(L, 48, 128, 1024)
    w_out = np.asarray(inp["w_out"], f32)
    wout = np.ascontiguousarray(w_out.reshape(L, 16, 128, 16, 128).transpose(0, 3, 2, 1, 4)).reshape(L, 16, 128, 2048)
    w_fi = np.asarray(inp["w_ffn_in"], f32)
    t = w_fi.reshape(L, 16, 128, 2, 44, 128).transpose(0, 4, 3, 2, 1, 5)
    wfi = np.ascontiguousarray(t).reshape(L, 88, 128, 2048)
    w_fo = np.asarray(inp["w_ffn_out"], f32)
    wfo = np.ascontiguousarray(w_fo.reshape(L, 44, 128, 16, 128).transpose(0, 3, 2, 1, 4)).reshape(L, 16, 128, 5632)
    pp = np.zeros((128, NPP), f32)

    def put(key, arr):
        c = PPM[key]
        pp[:, c:c + arr.shape[1]] = arr
    put("ln_in_g", _fm(inp["ln_in_g"], 16))
    put("ln_in_b", _fm(inp["ln_in_b"], 16))
    for l in range(L):
        put(("gate_bias", l), _fm(np.asarray(inp["gate_bias"])[l], 48))
        ca = np.asarray(inp["conv_a_w"], f32)[l]
        put(("conv_a_w", l), np.ascontiguousarray(ca.reshape(3, 8, 128).transpose(2, 1, 0)).reshape(128, 24))
        put(("sg_ln_g", l), _fm(np.asarray(inp["sg_ln_g"])[l], 8))
        put(("sg_ln_b", l), _fm(np.asarray(inp["sg_ln_b"])[l], 8))
        cw = np.asarray(inp["cc_conv_w"], f32)[l]
        put(("cc_conv_w", l), np.ascontiguousarray(cw.reshape(31, 8, 128).transpose(2, 1, 0)).reshape(128, 248))
        put(("cc_conv_b", l), _fm(np.asarray(inp["cc_conv_b"])[l], 8))
        put(("cc_ln_g", l), _fm(np.asarray(inp["cc_ln_g"])[l], 8))
        put(("cc_ln_b", l), _fm(np.asarray(inp["cc_ln_b"])[l], 8))
        put(("ln_mix_g", l), _fm(np.asarray(inp["ln_mix_g"])[l], 16))
        put(("ln_mix_b", l), _fm(np.asarray(inp["ln_mix_b"])[l], 16))
        put(("ln_ffn_g", l), _fm(np.asarray(inp["ln_ffn_g"])[l], 16))
        put(("ln_ffn_b", l), _fm(np.asarray(inp["ln_ffn_b"])[l], 16))
    sg_w = np.asarray(inp["sg_w"], f32)
    sgwT = np.ascontiguousarray(sg_w.transpose(0, 3, 1, 2)).reshape(L, 128, 1024)
    sgb = np.ascontiguousarray(np.asarray(inp["sg_b"], f32).reshape(L, 1024))
    tri = np.concatenate([np.triu(np.ones((128, 128), f32)), np.eye(128, dtype=f32)], axis=1)
    return {"win": win, "wvv": wvv, "wbr": wbr, "wout": wout, "wfi": wfi, "wfo": wfo, "pp": pp,
            "sgwT": sgwT, "sgb": sgb, "tri": tri}


def _prep_core(x, c):
    b, q = divmod(c, 4)
    t0 = q * OWN
    xs = np.zeros((TL, D_MODEL), np.float32)
    lo = t0 - HALO
    if lo >= 0:
        xs[:] = x[b, lo:t0 + OWN]
        tm = np.ones((128, HALO), np.float32)
    else:
        xs[HALO:] = x[b, 0:OWN]
        tm = np.zeros((128, HALO), np.float32)
    xT = np.ascontiguousarray(xs.T).reshape(16, 128, TL)
    return {"xT": xT, "tmask": tm}


_NC_CACHE = {}


def kernel(**inputs):
    x = np.asarray(inputs["x"], np.float32)
    shared = _prep_shared(inputs)
    in_maps = []
    for c in range(NCORES):
        m = dict(shared)
        m.update(_prep_core(x, c))
        in_maps.append(m)
    if "nc" not in _NC_CACHE:
        _NC_CACHE["nc"] = KB().build()
    nc = _NC_CACHE["nc"]
    res = run_bass_kernel_spmd(nc, in_maps, core_ids=list(range(NCORES)))
    out = np.empty((2, SEQ, D_MODEL), np.float32)
    for c in range(NCORES):
        b, q = divmod(c, 4)
        o = np.asarray(res.results[c]["out"]).reshape(D_MODEL, OWN)
        out[b, q * OWN:(q + 1) * OWN, :] = o.T
    return out
```

```python
import numpy as np
import concourse.bass as bass
import concourse.mybir as mybir
from concourse.bass_utils import run_bass_kernel_spmd
from contextlib import ExitStack

F32 = mybir.dt.float32
BF16 = mybir.dt.bfloat16
AF = mybir.ActivationFunctionType
ALU = mybir.AluOpType

D_MODEL = 2048
SEQ = 4096
DEPTH = 4
D_FF = 5632
LN_EPS = 1e-5
ALPHA = float((2 * DEPTH) ** 0.25)
NCORES = 8
OWN = 1024
HALO = 256
TL = OWN + HALO
GT = 640
S_L = (64, 96, 128, 224)
S_V = (0, 128, 128, 256)
NS = 6
NSCR = 8
ENGS = ("pe", "act", "dve", "pool", "sp")


class Buf:
    __slots__ = ("name", "writer", "readers")

    def __init__(self, name):
        self.name = name
        self.writer = None
        self.readers = {}


class Op:
    __slots__ = ("eng", "fn", "deps", "sig", "idx", "dkey", "dval")

    def __init__(self, eng, fn):
        self.eng = eng
        self.fn = fn
        self.deps = []
        self.sig = False
        self.idx = 0
        self.dkey = None
        self.dval = 0


class Prog:
    def __init__(self, nc):
        self.nc = nc
        self.q = {e: [] for e in ENGS}
        self.dma_last = {}
        self.dma_cnt = {}

    def op(self, eng, fn, reads=(), writes=(), dkey=None):
        o = Op(eng, fn)
        o.dkey = dkey
        deps = {}

        def add(p, raw):
            if p is None or p is o:
                return
            if p.dkey is None and o.dkey is None and p.eng == eng:
                if eng == "pe" or not raw:
                    return
            deps[id(p)] = p

        for b in reads:
            add(b.writer, True)
        for b in writes:
            add(b.writer, True)
            for r in b.readers.values():
                add(r, False)
        if dkey is not None:
            add(self.dma_last.get(dkey), True)
            self.dma_last[dkey] = o
            self.dma_cnt[dkey] = self.dma_cnt.get(dkey, 0) + 1
            o.dval = 16 * self.dma_cnt[dkey]
        for p in deps.values():
            p.sig = True
        o.deps = list(deps.values())
        for b in writes:
            b.writer = o
            b.readers = {}
        for b in reads:
            b.readers[eng if dkey is None else ("dma", dkey)] = o
        self.q[eng].append(o)
        return o

    def emit(self, final_waits=()):
        nc = self.nc
        for e in ENGS:
            c = 0
            for o in self.q[e]:
                if o.dkey is None and o.sig:
                    c += 1
                    o.idx = c
        with ExitStack() as es:
            semh = {}
            for e in ENGS:
                semh[e] = es.enter_context(nc.semaphore("s_" + e))
            for k in self.dma_cnt:
                semh[("dma", k)] = es.enter_context(nc.semaphore("d_" + str(k)))
            block = es.enter_context(nc.Block())

            def token(p):
                if p.dkey is not None:
                    return ("dma", p.dkey), p.dval
                return p.eng, p.idx

            def body(ename):
                def f(eng):
                    seen = {}
                    for o in self.q[ename]:
                        need = {}
                        for d in o.deps:
                            s, v = token(d)
                            if seen.get(s, 0) < v and need.get(s, 0) < v:
                                need[s] = v
                        for s, v in need.items():
                            eng.wait_ge(semh[s], v)
                            seen[s] = v
                        inst = o.fn(eng)
                        if o.dkey is not None:
                            inst.then_inc(semh[("dma", o.dkey)], 16)
                        elif o.sig:
                            inst.then_inc(semh[ename], 1)
                    if ename == "sp":
                        for p in final_waits:
                            s, v = token(p)
                            eng.wait_ge(semh[s], v)
                return f

            block.tensor(body("pe"))
            block.scalar(body("act"))
            block.vector(body("dve"))
            block.gpsimd(body("pool"))
            block.sync(body("sp"))


def _pp_map():
    m = {}
    c = 0

    def put(name, n):
        nonlocal c
        m[name] = c
        c += n
    put("ln_in_g", 16)
    put("ln_in_b", 16)
    for l in range(DEPTH):
        put(("gate_bias", l), 48)
        put(("conv_a_w", l), 24)
        put(("sg_ln_g", l), 8)
        put(("sg_ln_b", l), 8)
        put(("cc_conv_w", l), 248)
        put(("cc_conv_b", l), 8)
        put(("cc_ln_g", l), 8)
        put(("cc_ln_b", l), 8)
        put(("ln_mix_g", l), 16)
        put(("ln_mix_b", l), 16)
        put(("ln_ffn_g", l), 16)
        put(("ln_ffn_b", l), 16)
    return m, c


PPM, NPP = _pp_map()


class KB:
    def __init__(self, depth=DEPTH, groups=(0, 1)):
        self.depth = depth
        self.groups = groups
        nc = self.nc = bass.Bass("TRN2", target_bir_lowering=False)
        self.P = Prog(nc)
        dt = nc.dram_tensor
        self.d_x = dt("xT", [16, 128, TL], F32, kind="ExternalInput").ap()
        self.d_tmask = dt("tmask", [128, HALO], F32, kind="ExternalInput").ap()
        self.d_win = dt("win", [DEPTH, 104, 128, 2048], F32, kind="ExternalInput").ap()
        self.d_wv = dt("wvv", [DEPTH, 8, 128, 2048], F32, kind="ExternalInput").ap()
        self.d_wbr = dt("wbr", [DEPTH, 48, 128, 1024], F32, kind="ExternalInput").ap()
        self.d_wout = dt("wout", [DEPTH, 16, 128, 2048], F32, kind="ExternalInput").ap()
        self.d_wfi = dt("wfi", [DEPTH, 88, 128, 2048], F32, kind="ExternalInput").ap()
        self.d_wfo = dt("wfo", [DEPTH, 16, 128, 5632], F32, kind="ExternalInput").ap()
        self.d_pp = dt("pp", [128, NPP], F32, kind="ExternalInput").ap()
        self.d_sgw = dt("sgwT", [DEPTH, 128, 1024], F32, kind="ExternalInput").ap()
        self.d_sgb = dt("sgb", [DEPTH, 1024], F32, kind="ExternalInput").ap()
        self.d_tri = dt("tri", [128, 256], F32, kind="ExternalInput").ap()
        self.d_out = dt("out", [16, 128, OWN], F32, kind="ExternalOutput").ap()

    def sb(self, name, shape, dtype):
        return self.es.enter_context(self.nc.sbuf_tensor("sb_" + name, shape, dtype))

    def scr(self):
        i = self.scr_i
        self.scr_i = (i + 1) % NSCR
        return self.scrt[i], self.B_scr[i]

    def newps(self):
        i = self.ps_i
        self.ps_i = (i + 1) % 4
        return self.pst[i], self.B_ps[i]

    def mkviews(self):
        c0, nb, n = self.c0, self.nb, self.n

        def sv(ap2d):
            if nb == 2:
                return ap2d[:, c0:GT].rearrange("p (b n) -> p b n", b=2)
            return ap2d[:, c0:GT]

        def pv(D):
            if nb == 2:
                return D[:, :, 0:n]
            return D[:, 0, 0:n]
        return sv, pv

    def ppc(self, key, col=0):
        c = PPM[key] + col
        return self.pp[:, c:c + 1]

    def bslot(self, s):
        return self.big[:, s * GT:(s + 1) * GT]

    def y_ap(self, br, j):
        return self.bslot(br * 8 + j), [self.B_big[br * 8 + j]]

    def merged_ap(self, j):
        return self.bslot(24 + j), [self.B_big[24 + j]]

    def co_ap(self, j):
        s = 24 + 2 * j
        return self.big[:, s * GT:(s + 2) * GT].bitcast(F32), [self.B_big[s], self.B_big[s + 1]]

    def vn_ap(self, c):
        col = 36 * GT + c * 1024
        s0 = col // GT
        s1 = (col + 1023) // GT
        return self.big[:, col:col + 1024], [self.B_big[s] for s in range(s0, s1 + 1)]

    def h_ap(self, i):
        return self.bslot(i), [self.B_big[i]]

    def wseq_build(self):
        seq = []
        for g in self.groups:
            for l in range(self.depth):
                chunks = list(range((S_V[l] if g == 0 else 0) // 128, GT // 128))
                passes = [chunks[:3], chunks[3:]] if len(chunks) > 3 else [chunks]
                for j in range(8):
                    for e in (40 + j, 48 + j):
                        seq.append((("in", g, l, e), self.d_win[l, e], 2048))
                for pi, pch in enumerate(passes):
                    for u in range(8):
                        seq.append((("v", g, l, pi, u), self.d_wv[l, u], 2048))
                for j in range(8):
                    for e in (8 + j, 16 + j, j):
                        seq.append((("in", g, l, e), self.d_win[l, e], 2048))
                for gg in range(8):
                    seq.append((("in", g, l, 24 + gg), self.d_win[l, 24 + gg], 2048))
                for j in range(16):
                    for n in range(3):
                        e = 56 + n * 16 + j
                        seq.append((("in", g, l, e), self.d_win[l, e], 2048))
                        seq.append((("br", g, l, n, j), self.d_wbr[l, n * 16 + j], 1024))
                for j in range(16):
                    seq.append((("out", g, l, j), self.d_wout[l, j], 2048))
                for i in range(44):
                    seq.append((("fi", g, l, 2 * i), self.d_wfi[l, 2 * i], 2048))
                    seq.append((("fi", g, l, 2 * i + 1), self.d_wfi[l, 2 * i + 1], 2048))
                for j in range(16):
                    for (k0, nk) in ((0, 16), (16, 16), (32, 12)):
                        seq.append((("fo", g, l, j, k0), self.d_wfo[l, j][:, k0 * 128:(k0 + nk) * 128], nk * 128))
        self.wseq = seq
        self.w_issued = 0
        self.w_pos = 0

    def wget(self, key, hold=False):
        seq = self.wseq
        assert seq[self.w_pos][0] == key, (seq[self.w_pos][0], key)
        while self.w_issued < min(len(seq), self.w_pos + (1 if hold else NS)):
            i = self.w_issued
            _, src, ncols = seq[i]
            s = i % NS
            self.P.op("pool", lambda e, s=s, src=src, ncols=ncols: e.dma_start(out=self.ring[s][:, 0:ncols], in_=src),
                      writes=[self.B_ring[s]], dkey="w%d" % s)
            self.w_issued += 1
        s = self.w_pos % NS
        self.w_pos += 1
        return self.ring[s], self.B_ring[s]

    def mm_fm(self, D, BD, wt, Bw, nk, act3, Bact, kbase=0, start=True, stop=True):
        c0, n, nb = self.c0, self.n, self.nb

        def f(e):
            r = None
            for b in range(nb):
                for k in range(nk):
                    r = e.matmul(D[:, b, 0:n], lhsT=wt[:, k * 128:(k + 1) * 128],
                                 rhs=act3[:, kbase + k, c0 + b * n:c0 + (b + 1) * n],
                                 start=(start and k == 0), stop=(stop and k == nk - 1))
            return r
        self.P.op("pe", f, reads=[Bw] + list(Bact), writes=[BD])

    def kouter(self, keys):
        P = self.P
        c0, n, nb = self.c0, self.n, self.nb
        xh3 = self.xh[:, :].rearrange("p (j t) -> p j t", j=16)
        units = []
        for i, key in enumerate(keys):
            wt, Bw = self.wget(key, hold=(i > 0))
            D, BD = self.newps()
            units.append((key, wt, Bw, D, BD))
        for k in range(16):
            def f(e, k=k):
                r = None
                for (_, wt, _, D, _) in units:
                    for b in range(nb):
                        r = e.matmul(D[:, b, 0:n], lhsT=wt[:, k * 128:(k + 1) * 128],
                                     rhs=xh3[:, k, c0 + b * n:c0 + (b + 1) * n], start=(k == 0), stop=(k == 15))
                return r
            P.op("pe", f, reads=[self.B_xh[k]] + [u[2] for u in units], writes=[u[4] for u in units])
        for u in units:
            self.pre[u[0]] = (u[3], u[4])

    def unit_xh(self, key):
        if key in self.pre:
            return self.pre.pop(key)
        wt, Bw = self.wget(key)
        D, BD = self.newps()
        xh3 = self.xh[:, :].rearrange("p (j t) -> p j t", j=16)
        self.mm_fm(D, BD, wt, Bw, 16, xh3, self.B_xh)
        return D, BD

    def st_begin(self):
        self.st_n = 0

    def st_add(self, tap, tb):
        P = self.P
        c0 = self.c0
        accs, Bas = self.lnt[0], self.B_lnt[0]
        accq, Baq = self.lnt[1], self.B_lnt[1]
        j = self.st_n
        self.st_n += 1
        if j == 0:
            self.st_first = (tap, tb)
            P.op("act", lambda e: e.activation(out=accq[:, c0:GT], in_=tap[:, c0:GT], func=AF.Square),
                 reads=list(tb), writes=[Baq])
            return
        if j == 1:
            t0ap, t0b = self.st_first
            P.op("dve", lambda e: e.tensor_tensor(out=accs[:, c0:GT], in0=t0ap[:, c0:GT], in1=tap[:, c0:GT], op=ALU.add),
                 reads=list(tb) + list(t0b), writes=[Bas])
        else:
            P.op("dve", lambda e: e.tensor_tensor(out=accs[:, c0:GT], in0=accs[:, c0:GT], in1=tap[:, c0:GT], op=ALU.add),
                 reads=list(tb) + [Bas], writes=[Bas])
        sq, Bsq = self.scr()
        P.op("act", lambda e: e.activation(out=sq[:, c0:GT], in_=tap[:, c0:GT], func=AF.Square),
             reads=list(tb), writes=[Bsq])
        P.op("dve", lambda e: e.tensor_tensor(out=accq[:, c0:GT], in0=accq[:, c0:GT], in1=sq[:, c0:GT], op=ALU.add),
             reads=[Bsq, Baq], writes=[Baq])

    def st_finish(self, nfeat):
        P = self.P
        c0, n, nb = self.c0, self.n, self.nb
        sv, pv = self.mkviews()
        accs, Bas = self.lnt[0], self.B_lnt[0]
        accq, Baq = self.lnt[1], self.B_lnt[1]
        DS, BS = self.newps()
        DQ, BQ = self.newps()

        def fs(e):
            r = None
            for b in range(nb):
                r = e.matmul(DS[:, b, 0:n], lhsT=self.ones32[:, :], rhs=accs[:, c0 + b * n:c0 + (b + 1) * n], start=True, stop=True)
            return r
        P.op("pe", fs, reads=[Bas, self.B_const], writes=[BS])

        def fq(e):
            r = None
            for b in range(nb):
                r = e.matmul(DQ[:, b, 0:n], lhsT=self.ones32[:, :], rhs=accq[:, c0 + b * n:c0 + (b + 1) * n], start=True, stop=True)
            return r
        P.op("pe", fq, reads=[Baq, self.B_const], writes=[BQ])
        inv = 1.0 / nfeat
        mean, Bm = self.lnt[0], self.B_lnt[0]
        rstd, Br = self.lnt[1], self.B_lnt[1]
        t1, Bt1 = self.lnt[2], self.B_lnt[2]
        P.op("dve", lambda e: e.tensor_scalar(out=sv(mean), in0=pv(DS), scalar1=inv, scalar2=None, op0=ALU.mult),
             reads=[BS], writes=[Bm])
        P.op("dve", lambda e: e.tensor_tensor(out=t1[:, c0:GT], in0=mean[:, c0:GT], in1=mean[:, c0:GT], op=ALU.mult),
             reads=[Bm], writes=[Bt1])
        P.op("dve", lambda e: e.scalar_tensor_tensor(out=sv(rstd), in0=pv(DQ), scalar=inv, in1=sv(t1),
                                                     op0=ALU.mult, op1=ALU.subtract),
             reads=[BQ, Bt1], writes=[Br])
        P.op("act", lambda e: e.activation(out=t1[:, c0:GT], in_=rstd[:, c0:GT], func=AF.Sqrt, bias=self.epsc[:, 0:1], scale=1.0),
             reads=[Br, self.B_const], writes=[Bt1])

        def recip():
            P.op("dve", lambda e: e.reciprocal(out=rstd[:, c0:GT], in_=t1[:, c0:GT]),
                 reads=[Bt1], writes=[Br])
        self.ln_mean = (mean, Bm)
        return DS, BS, -inv, rstd, Br, recip

    def ln_apply(self, tiles, gkey, nfeat, outs):
        P = self.P
        c0 = self.c0
        sv, pv = self.mkviews()
        DS, BS, ninv, rstd, Br, recip = self.st_finish(nfeat)
        mean, Bm = self.ln_mean
        nt = len(tiles)
        for b0 in range(0, nt, 8):
            ds = []
            for j in range(b0, min(nt, b0 + 8)):
                tap, tb = tiles[j]
                d, Bd = self.scr()
                ds.append((j, d, Bd))
                P.op("dve", lambda e, tap=tap, d=d: e.tensor_tensor(out=d[:, c0:GT], in0=tap[:, c0:GT], in1=mean[:, c0:GT], op=ALU.subtract),
                     reads=list(tb) + [Bm], writes=[Bd])
            if b0 == 0:
                recip()
            for (j, d, Bd) in ds:
                P.op("dve", lambda e, d=d, j=j: e.scalar_tensor_tensor(out=d[:, c0:GT], in0=d[:, c0:GT], scalar=self.ppc(gkey, j),
                                                                     in1=rstd[:, c0:GT], op0=ALU.mult, op1=ALU.mult),
                     reads=[Bd, Br, self.B_pp], writes=[Bd])
                outs(j, d, Bd)

    def ln_x(self, gkey, bkey, accumulated=False, final=False, src=None):
        P = self.P
        c0 = self.c0
        sv, pv = self.mkviews()
        if src is None:
            src = [(self.x32[:, j * GT:(j + 1) * GT], [self.B_x32[j]]) for j in range(16)]
        if not accumulated:
            self.st_begin()
            for j in range(16):
                self.st_add(*src[j])
        DS, BS, ninv, rstd, Br, recip = self.st_finish(D_MODEL)
        mean, Bm = self.ln_mean
        for j in range(4, 16):
            xj = self.x32[:, j * GT:(j + 1) * GT]
            sj, sb_ = src[j]
            P.op("pool", lambda e, xj=xj, sj=sj: e.tensor_tensor(out=xj[:, c0:GT], in0=sj[:, c0:GT], in1=mean[:, c0:GT], op=ALU.subtract),
                 reads=list(sb_) + [Bm], writes=[self.B_x32[j]])
        for j in range(4):
            xj = self.x32[:, j * GT:(j + 1) * GT]
            sj, sb_ = src[j]
            P.op("dve", lambda e, xj=xj, sj=sj: e.scalar_tensor_tensor(out=sv(xj), in0=pv(DS), scalar=ninv, in1=sv(sj),
                                                                      op0=ALU.mult, op1=ALU.add),
                 reads=list(sb_) + [BS], writes=[self.B_x32[j]])
            if j == 3:
                recip()
        for j in range(16):
            xj = self.x32[:, j * GT:(j + 1) * GT]
            hj = self.xh[:, j * GT:(j + 1) * GT]
            P.op("dve", lambda e, xj=xj, j=j: e.scalar_tensor_tensor(out=xj[:, c0:GT], in0=xj[:, c0:GT], scalar=self.ppc(gkey, j),
                                                                   in1=rstd[:, c0:GT], op0=ALU.mult, op1=ALU.mult),
                 reads=[self.B_x32[j], Br, self.B_pp], writes=[self.B_x32[j]])
            if final:
                P.op("act", lambda e, xj=xj, j=j: e.activation(out=xj[:, c0:GT], in_=xj[:, c0:GT], func=AF.Identity,
                                                             bias=self.ppc(bkey, j), scale=1.0),
                     reads=[self.B_x32[j], self.B_pp], writes=[self.B_x32[j]])
                continue
            P.op("act", lambda e, xj=xj, hj=hj, j=j: e.activation(out=hj[:, c0:GT], in_=xj[:, c0:GT], func=AF.Identity,
                                                                 bias=self.ppc(bkey, j), scale=1.0),
                 reads=[self.B_x32[j], self.B_pp], writes=[self.B_xh[j]])

            def late(xj=xj, j=j):
                P.op("act", lambda e: e.activation(out=xj[:, c0:GT], in_=xj[:, c0:GT], func=AF.Identity,
                                                   bias=self.ppc(bkey, j), scale=1.0),
                     reads=[self.B_x32[j], self.B_pp], writes=[self.B_x32[j]])
            self.deferred.append(late)

    def run_deferred(self, k=1):
        while k > 0 and self.deferred:
            self.deferred.pop(0)()
            k -= 1

    def flush_deferred(self):
        self.run_deferred(len(self.deferred))

    def layer(self, g, l):
        P = self.P
        c0 = self.c0 = (S_L[l] if g == 0 else 0)
        Tg = GT - c0
        self.nb = nb = 2 if Tg > 512 else 1
        self.n = n = Tg // nb
        sv, pv = self.mkviews()
        xh3 = self.xh[:, :].rearrange("p (j t) -> p j t", j=16)
        big3 = self.big[:, :].rearrange("p (j t) -> p j t", j=44)
        nmask = max(0, HALO - c0) if g == 0 else 0

        P.op("pool", lambda e: e.dma_start(out=self.wTm[:, :], in_=self.d_sgw[l]), writes=[self.B_wTm], dkey="sgw")
        P.op("dve", lambda e: e.tensor_tensor(out=self.wTm[:, :].rearrange("p (g t) -> p g t", g=8),
                                              in0=self.wTm[:, :].rearrange("p (g t) -> p g t", g=8),
                                              in1=self.tri[:, 0:128].unsqueeze(1).to_broadcast([128, 8, 128]), op=ALU.mult),
             reads=[self.B_wTm, self.B_const], writes=[self.B_wTm])
        P.op("sp", lambda e: e.dma_start(out=self.Cc[:, :], in_=self.d_sgb[l].partition_broadcast(128)),
             writes=[self.B_Cc], dkey="sgb")
        def c_stage1(j):
            DA, BA = self.unit_xh(("in", g, l, 40 + j))
            DG, BG = self.unit_xh(("in", g, l, 48 + j))
            s1, Bs1 = self.scr()
            P.op("act", lambda e, s1=s1, DG=DG: e.activation(out=sv(s1), in_=pv(DG), func=AF.Sigmoid), reads=[BG], writes=[Bs1])
            self.run_deferred(2)
            ci = self.cv_i
            self.cv_i = (ci + 1) % 2
            yp, Byp = self.ygb[ci], self.B_ygb[ci]
            dg, Bdg = self.diag[ci], self.B_diag[ci]
            yv = yp[:, 30:30 + GT]
            tb = (l * 8 + j) * 30
            wc = PPM[("cc_conv_w", l)] + j * 31
            P.op("dve", lambda e, dg=dg, wc=wc: e.tensor_tensor(
                out=dg[:, :].rearrange("p (k m) -> p k m", k=31),
                in0=self.tri[:, 128:256].unsqueeze(1).to_broadcast([128, 31, 128]),
                in1=self.pp[:, wc:wc + 31].unsqueeze(2).to_broadcast([128, 31, 128]), op=ALU.mult),
                reads=[self.B_const, self.B_pp], writes=[Bdg])
            if g == 0:
                P.op("dve", lambda e, yp=yp: e.memset(yp[:, c0:c0 + 30], 0.0), writes=[Byp])
            else:
                P.op("dve", lambda e, yp=yp, tb=tb: e.tensor_copy(out=yp[:, 0:30], in_=self.ctail[:, tb:tb + 30]),
                     reads=[self.B_ctail[l * 8 + j]], writes=[Byp])
            P.op("dve", lambda e, yv=yv, s1=s1, DA=DA: e.tensor_tensor(out=sv(yv), in0=sv(s1), in1=pv(DA), op=ALU.mult),
                 reads=[Bs1, BA], writes=[Byp])
            if nmask:
                P.op("dve", lambda e, yv=yv: e.tensor_tensor(out=yv[:, c0:HALO], in0=yv[:, c0:HALO], in1=self.tmask[:, c0:HALO], op=ALU.mult),
                     reads=[Byp, self.B_const], writes=[Byp])
            if g == 0:
                P.op("dve", lambda e, yp=yp, tb=tb: e.tensor_copy(out=self.ctail[:, tb:tb + 30], in_=yp[:, GT:GT + 30]),
                     reads=[Byp], writes=[self.B_ctail[l * 8 + j]])
            return (j, yp, Byp, dg, Bdg)

        def c_stage2(ctx):
            j, yp, Byp, dg, Bdg = ctx
            DV, BV = self.newps()

            def fcv(e, DV=DV, dg=dg, yp=yp):
                r = None
                for b in range(nb):
                    for k in range(31):
                        r = e.matmul(DV[:, b, 0:n], lhsT=dg[:, k * 128:(k + 1) * 128],
                                     rhs=yp[:, c0 + b * n + k:c0 + b * n + k + n], start=(k == 0), stop=(k == 30))
                return r
            P.op("pe", fcv, reads=[Bdg, Byp], writes=[BV])
            co, Bco = self.co_ap(j)
            P.op("act", lambda e, DV=DV, co=co, j=j: e.activation(out=sv(co), in_=pv(DV), func=AF.Identity,
                                                                 bias=self.ppc(("cc_conv_b", l), j), scale=1.0),
                 reads=[BV, self.B_pp], writes=Bco)
            if j == 0:
                self.st_begin()
            self.st_add(co, Bco)

        self.kouter([("in", g, l, 40), ("in", g, l, 48), ("in", g, l, 41)])
        pend = None
        for j in range(8):
            ctx = c_stage1(j)
            if pend is not None:
                c_stage2(pend)
            pend = ctx
        c_stage2(pend)
        DR, BR = self.newps()

        def frs(e):
            r = None
            for b in range(2):
                r = e.matmul(DR[:, b, :], lhsT=self.ones16[:, :], rhs=self.wTm[:, b * 512:(b + 1) * 512], start=True, stop=True)
            return r
        P.op("pe", frs, reads=[self.B_wTm, self.B_const], writes=[BR])
        for gg in range(8):
            P.op("dve", lambda e, gg=gg: e.scalar_tensor_tensor(
                out=self.Cc[:, gg * 128:(gg + 1) * 128], in0=DR[:, gg // 4, (gg % 4) * 128:(gg % 4 + 1) * 128],
                scalar=self.ppc(("sg_ln_b", l), gg), in1=self.Cc[:, gg * 128:(gg + 1) * 128], op0=ALU.mult, op1=ALU.add),
                reads=[BR, self.B_pp, self.B_Cc], writes=[self.B_Cc])

        tiles = [self.co_ap(j) for j in range(8)]

        def outs_c(j, d, Bd):
            yc, Byc = self.y_ap(2, j)
            P.op("act", lambda e: e.activation(out=yc[:, c0:GT], in_=d[:, c0:GT], func=AF.Silu,
                                               bias=self.ppc(("cc_ln_b", l), j), scale=1.0),
                 reads=[Bd, self.B_pp], writes=Byc)
        self.ln_apply(tiles, ("cc_ln_g", l), 1024, outs_c)

        chunks = list(range((S_V[l] if g == 0 else 0) // 128, GT // 128))
        passes = [chunks[:3], chunks[3:]] if len(chunks) > 3 else [chunks]
        for pi, pch in enumerate(passes):
            Ds = [self.newps() for _ in pch]
            for u in range(8):
                wt, Bw = self.wget(("v", g, l, pi, u))
                for ci, c in enumerate(pch):
                    D, BD = Ds[ci]

                    def fv(e, wt=wt, D=D, c=c, u=u):
                        r = None
                        for kk in range(2):
                            k = 2 * u + kk
                            for cb in range(2):
                                r = e.matmul(D[:, cb, :], lhsT=xh3[:, k, c * 128:(c + 1) * 128],
                                             rhs=wt[:, kk * 1024 + cb * 512:kk * 1024 + (cb + 1) * 512],
                                             start=(k == 0), stop=(k == 15))
                        return r
                    P.op("pe", fv, reads=[Bw, self.B_xh[2 * u], self.B_xh[2 * u + 1]], writes=[BD])
            for ci, c in enumerate(pch):
                D, BD = Ds[ci]
                si = self.vs_i
                self.vs_i = (si + 1) % 2
                st = self.vst[si]
                Bst = self.B_vst[si]
                P.op("dve", lambda e, D=D, st=st: e.bn_stats(out=st[:, 0:6], in_=D[:, 0, :]), reads=[BD], writes=[Bst])
                P.op("dve", lambda e, D=D, st=st: e.bn_stats(out=st[:, 6:12], in_=D[:, 1, :]), reads=[BD], writes=[Bst])
                P.op("dve", lambda e, st=st: e.bn_aggr(out=st[:, 12:14], in_=st[:, 0:12]), reads=[Bst], writes=[Bst])
                P.op("dve", lambda e, st=st: e.tensor_scalar(out=st[:, 14:15], in0=st[:, 13:14], scalar1=LN_EPS, scalar2=None, op0=ALU.add),
                     reads=[Bst], writes=[Bst])
                P.op("act", lambda e, st=st: e.activation(out=st[:, 15:16], in_=st[:, 14:15], func=AF.Sqrt), reads=[Bst], writes=[Bst])
                P.op("dve", lambda e, st=st: e.reciprocal(out=st[:, 16:17], in_=st[:, 15:16]), reads=[Bst], writes=[Bst])
                vn, Bvn = self.vn_ap(c)
                P.op("dve", lambda e, D=D, st=st, vn=vn: e.tensor_scalar(
                    out=vn.rearrange("p (b n) -> p b n", b=2), in0=D[:, :, :], scalar1=st[:, 12:13], scalar2=st[:, 16:17],
                    op0=ALU.subtract, op1=ALU.mult), reads=[BD, Bst], writes=Bvn)

        for j in range(8):
            wt, Bw = self.wget(("in", g, l, 8 + j))
            DC, BC = self.newps()
            self.mm_fm(DC, BC, wt, Bw, 16, xh3, self.B_xh)
            wt, Bw = self.wget(("in", g, l, 16 + j))
            DH, BH = self.newps()
            self.mm_fm(DH, BH, wt, Bw, 16, xh3, self.B_xh)
            wt, Bw = self.wget(("in", g, l, j))
            DB, BB = self.newps()
            self.mm_fm(DB, BB, wt, Bw, 16, xh3, self.B_xh)
            s1, Bs1 = self.scr()
            P.op("act", lambda e, s1=s1, DC=DC: e.activation(out=sv(s1), in_=pv(DC), func=AF.Copy), reads=[BC], writes=[Bs1])
            pp_, Bpp_ = self.scr()
            pv2 = pp_[:, 2:2 + GT]
            if g == 0:
                P.op("dve", lambda e, pp_=pp_: e.memset(pp_[:, c0:c0 + 2], 0.0), writes=[Bpp_])
            else:
                P.op("dve", lambda e, pp_=pp_, j=j: e.tensor_copy(out=pp_[:, 0:2], in_=self.atail[:, (l * 8 + j) * 2:(l * 8 + j) * 2 + 2]),
                     reads=[self.B_atail[l * 8 + j]], writes=[Bpp_])
            P.op("dve", lambda e, pv2=pv2, s1=s1, DH=DH: e.tensor_tensor(out=sv(pv2), in0=sv(s1), in1=pv(DH), op=ALU.mult),
                 reads=[Bs1, BH], writes=[Bpp_])
            if nmask:
                P.op("dve", lambda e, pv2=pv2: e.tensor_tensor(out=pv2[:, c0:HALO], in0=pv2[:, c0:HALO], in1=self.tmask[:, c0:HALO], op=ALU.mult),
                     reads=[Bpp_, self.B_const], writes=[Bpp_])
            if g == 0:
                P.op("dve", lambda e, pp_=pp_, j=j: e.tensor_copy(out=self.atail[:, (l * 8 + j) * 2:(l * 8 + j) * 2 + 2], in_=pp_[:, GT:GT + 2]),
                     reads=[Bpp_], writes=[self.B_atail[l * 8 + j]])
            acc, Bacc = self.scr()
            wc = PPM[("conv_a_w", l)] + j * 3
            P.op("dve", lambda e, acc=acc, pp_=pp_, wc=wc: e.tensor_scalar(out=acc[:, c0:GT], in0=pp_[:, c0:GT], scalar1=self.pp[:, wc:wc + 1],
                                                                          scalar2=None, op0=ALU.mult),
                 reads=[Bpp_, self.B_pp], writes=[Bacc])
            for k in (1, 2):
                P.op("dve", lambda e, acc=acc, pp_=pp_, wc=wc, k=k: e.scalar_tensor_tensor(
                    out=acc[:, c0:GT], in0=pp_[:, c0 + k:GT + k], scalar=self.pp[:, wc + k:wc + k + 1], in1=acc[:, c0:GT],
                    op0=ALU.mult, op1=ALU.add), reads=[Bpp_, Bacc, self.B_pp], writes=[Bacc])
            ya, Bya = self.y_ap(0, j)
            P.op("dve", lambda e, acc=acc, DB=DB, ya=ya: e.tensor_tensor(out=sv(ya), in0=sv(acc), in1=pv(DB), op=ALU.mult),
                 reads=[Bacc, BB], writes=Bya)

        for gg in range(8):
            wt, Bw = self.wget(("in", g, l, 24 + gg))
            DU, BU = self.newps()
            self.mm_fm(DU, BU, wt, Bw, 16, xh3, self.B_xh)
            DM, BM = self.newps()

            def fm(e, gg=gg, DM=DM):
                r = None
                for c in chunks:
                    a = c * 128 - c0
                    bnd = [max(a, 0), a + 128]
                    if nb == 2 and bnd[0] < n < a + 128:
                        bnd = [bnd[0], n, a + 128]
                    for i in range(len(bnd) - 1):
                        lo, hi = bnd[i], bnd[i + 1]
                        b = lo // n
                        vn, _ = self.vn_ap(c)
                        r = e.matmul(DM[:, b, lo - b * n:hi - b * n], lhsT=vn[:, gg * 128:(gg + 1) * 128],
                                     rhs=self.wTm[:, gg * 128 + (lo - a):gg * 128 + (hi - a)], start=True, stop=True,
                                     skip_group_check=True)
                return r
            rd = [self.B_wTm]
            for c in chunks:
                rd += self.vn_ap(c)[1]
            P.op("pe", fm, reads=rd, writes=[BM])
            t, Bt = self.scr()
            nch = len(chunks)
            if nb == 1 or True:
                for b in range(nb):
                    lo = c0 + b * n
                    P.op("dve", lambda e, gg=gg, DM=DM, t=t, b=b, lo=lo: e.tensor_scalar(
                        out=t[:, lo:lo + n], in0=DM[:, b, 0:n], scalar1=self.ppc(("sg_ln_g", l), gg), scalar2=None, op0=ALU.mult),
                        reads=[BM, self.B_pp], writes=[Bt])
            cs0 = ((c0 + 127) // 128) * 128
            nch = (GT - cs0) // 128
            if cs0 > c0 and chunks[0] * 128 < cs0:
                P.op("dve", lambda e, gg=gg, t=t, cs0=cs0: e.tensor_tensor(
                    out=t[:, c0:cs0], in0=t[:, c0:cs0],
                    in1=self.Cc[:, gg * 128 + 128 - (cs0 - c0):(gg + 1) * 128], op=ALU.add),
                    reads=[Bt, self.B_Cc], writes=[Bt])
            P.op("dve", lambda e, gg=gg, t=t, cs0=cs0: e.tensor_tensor(
                out=t[:, cs0:GT].rearrange("p (c t) -> p c t", t=128), in0=t[:, cs0:GT].rearrange("p (c t) -> p c t", t=128),
                in1=self.Cc[:, gg * 128:(gg + 1) * 128].unsqueeze(1).to_broadcast([128, nch, 128]), op=ALU.add),
                reads=[Bt, self.B_Cc], writes=[Bt])
            yb, Byb = self.y_ap(1, gg)
            P.op("dve", lambda e, t=t, DU=DU, yb=yb: e.tensor_tensor(out=sv(yb), in0=sv(t), in1=pv(DU), op=ALU.mult),
                 reads=[Bt, BU], writes=Byb)

        for j in range(16):
            m, Bm_ = self.scr()
            mg, Bmg = self.merged_ap(j)
            for nbr in range(3):
                wt, Bw = self.wget(("in", g, l, 56 + nbr * 16 + j))
                DG, BG = self.newps()
                self.mm_fm(DG, BG, wt, Bw, 16, xh3, self.B_xh)
                wt, Bw = self.wget(("br", g, l, nbr, j))
                DP, BP = self.newps()
                self.mm_fm(DP, BP, wt, Bw, 8, big3, [self.B_big[nbr * 8 + k] for k in range(8)], kbase=nbr * 8)
                gs, Bgs = self.scr()
                P.op("act", lambda e, gs=gs, DG=DG, nbr=nbr, j=j: e.activation(out=sv(gs), in_=pv(DG), func=AF.Sigmoid,
                                                                             bias=self.ppc(("gate_bias", l), nbr * 16 + j), scale=1.0),
                     reads=[BG, self.B_pp], writes=[Bgs])
                if nbr == 0:
                    P.op("dve", lambda e, m=m, gs=gs, DP=DP: e.tensor_tensor(out=sv(m), in0=sv(gs), in1=pv(DP), op=ALU.mult),
                         reads=[Bgs, BP], writes=[Bm_])
                else:
                    P.op("dve", lambda e, gs=gs, DP=DP: e.tensor_tensor(out=sv(gs), in0=sv(gs), in1=pv(DP), op=ALU.mult),
                         reads=[Bgs, BP], writes=[Bgs])
                    if nbr == 1:
                        P.op("dve", lambda e, m=m, gs=gs: e.tensor_tensor(out=m[:, c0:GT], in0=m[:, c0:GT], in1=gs[:, c0:GT], op=ALU.add),
                             reads=[Bm_, Bgs], writes=[Bm_])
                    else:
                        P.op("dve", lambda e, m=m, gs=gs, mg=mg: e.tensor_tensor(out=mg[:, c0:GT], in0=m[:, c0:GT], in1=gs[:, c0:GT], op=ALU.add),
                             reads=[Bm_, Bgs], writes=Bmg)

        self.flush_deferred()
        for j in range(16):
            wt, Bw = self.wget(("out", g, l, j))
            DO, BO = self.newps()
            self.mm_fm(DO, BO, wt, Bw, 16, big3, [self.B_big[24 + k] for k in range(16)], kbase=24)
            xj = self.x32[:, j * GT:(j + 1) * GT]
            P.op("dve", lambda e, xj=xj, DO=DO: e.scalar_tensor_tensor(out=sv(xj), in0=sv(xj), scalar=ALPHA, in1=pv(DO),
                                                                      op0=ALU.mult, op1=ALU.add),
                 reads=[self.B_x32[j], BO], writes=[self.B_x32[j]])
            if j == 0:
                self.st_begin()
            self.st_add(xj, [self.B_x32[j]])
        self.ln_x(("ln_mix_g", l), ("ln_mix_b", l), accumulated=True)

        self.kouter([("fi", g, l, 0), ("fi", g, l, 1), ("fi", g, l, 2)])
        for i in range(44):
            DG, BG = self.unit_xh(("fi", g, l, 2 * i))
            DU, BU = self.unit_xh(("fi", g, l, 2 * i + 1))
            s1, Bs1 = self.scr()
            P.op("act", lambda e, s1=s1, DG=DG: e.activation(out=sv(s1), in_=pv(DG), func=AF.Silu), reads=[BG], writes=[Bs1])
            self.run_deferred(1)
            hi, Bhi = self.h_ap(i)
            P.op("dve", lambda e, s1=s1, DU=DU, hi=hi: e.tensor_tensor(out=sv(hi), in0=sv(s1), in1=pv(DU), op=ALU.mult),
                 reads=[Bs1, BU], writes=Bhi)
        self.flush_deferred()
        for j in range(16):
            DF, BF = self.newps()
            for (k0, nk) in ((0, 16), (16, 16), (32, 12)):
                wt, Bw = self.wget(("fo", g, l, j, k0))
                self.mm_fm(DF, BF, wt, Bw, nk, big3, [self.B_big[k0 + k] for k in range(nk)], kbase=k0,
                           start=(k0 == 0), stop=(k0 == 32))
            xj = self.x32[:, j * GT:(j + 1) * GT]
            P.op("dve", lambda e, xj=xj, DF=DF: e.scalar_tensor_tensor(out=sv(xj), in0=sv(xj), scalar=ALPHA, in1=pv(DF),
                                                                      op0=ALU.mult, op1=ALU.add),
                 reads=[self.B_x32[j], BF], writes=[self.B_x32[j]])
            if j == 0:
                self.st_begin()
            self.st_add(xj, [self.B_x32[j]])
        if l == self.depth - 1 and self.prefetch_next is not None:
            self.prefetch_next()
            self.prefetch_next = None
        self.ln_x(("ln_ffn_g", l), ("ln_ffn_b", l), accumulated=True, final=(l == self.depth - 1))

    def build(self):
        nc = self.nc
        P = self.P
        with ExitStack() as es:
            self.es = es
            self.x32 = self.sb("x32", [128, 16 * GT], F32)
            self.xh = self.sb("xh", [128, 16 * GT], BF16)
            self.big = self.sb("big", [128, 44 * GT], BF16)
            self.ring = [self.sb("ring%d" % i, [128, 2048], BF16) for i in range(NS)]
            self.scrt = [self.sb("scr%d" % i, [128, 704], F32) for i in range(NSCR)]
            self.ygb = [self.sb("ygb%d" % i, [128, 30 + GT + 2], BF16) for i in range(2)]
            self.diag = [self.sb("diag%d" % i, [128, 31 * 128], BF16) for i in range(2)]
            self.B_ygb = [Buf("ygb%d" % i) for i in range(2)]
            self.B_diag = [Buf("diag%d" % i) for i in range(2)]
            self.cv_i = 0
            self.pp = self.sb("pp", [128, NPP], F32)
            self.ones32 = self.sb("ones32", [128, 128], F32)
            self.ones16 = self.sb("ones16", [128, 128], BF16)
            self.epsc = self.sb("epsc", [128, 2], F32)
            self.tri = self.sb("tri", [128, 256], F32)
            self.tmask = self.sb("tmask", [128, HALO], F32)
            self.wTm = self.sb("wTm", [128, 1024], BF16)
            self.Cc = self.sb("Cc", [128, 1024], F32)
            self.ctail = self.sb("ctail", [128, DEPTH * 8 * 30], BF16)
            self.atail = self.sb("atail", [128, DEPTH * 8 * 2], F32)
            self.vst = [self.sb("vst%d" % i, [128, 24], F32) for i in range(2)]
            self.lnt = [self.sb("lnt%d" % i, [128, GT], F32) for i in range(3)]
            self.B_lnt = [Buf("lnt%d" % i) for i in range(3)]
            self.pst = [es.enter_context(nc.psum_tensor("ps%d" % i, [128, 2, 512], F32)) for i in range(4)]
            self.B_x32 = [Buf("x32_%d" % j) for j in range(16)]
            self.B_xh = [Buf("xh_%d" % j) for j in range(16)]
            self.B_big = [Buf("big_%d" % j) for j in range(44)]
            self.B_ring = [Buf("ring_%d" % j) for j in range(NS)]
            self.B_scr = [Buf("scr_%d" % j) for j in range(NSCR)]
            self.B_ps = [Buf("ps_%d" % j) for j in range(4)]
            self.B_pp = Buf("pp")
            self.B_const = Buf("const")
            self.B_wTm = Buf("wTm")
            self.B_Cc = Buf("Cc")
            self.B_ctail = [Buf("ct%d" % j) for j in range(DEPTH * 8)]
            self.B_atail = [Buf("at%d" % j) for j in range(DEPTH * 8)]
            self.B_vst = [Buf("vst%d" % j) for j in range(2)]
            self.deferred = []
            self.pre = {}
            self.scr_i = 0
            self.ps_i = 0
            self.vs_i = 0
            self.wseq_build()

            P.op("sp", lambda e: e.dma_start(out=self.pp[:, :], in_=self.d_pp), writes=[self.B_pp], dkey="pp")
            P.op("sp", lambda e: e.dma_start(out=self.tri[:, :], in_=self.d_tri), writes=[self.B_const], dkey="c0")
            P.op("sp", lambda e: e.dma_start(out=self.tmask[:, :], in_=self.d_tmask), writes=[self.B_const], dkey="c0")
            P.op("dve", lambda e: e.memset(self.ones32[:, :], 1.0), writes=[self.B_const])
            P.op("dve", lambda e: e.memset(self.ones16[:, :], 1.0), writes=[self.B_const])
            P.op("dve", lambda e: e.memset(self.epsc[:, :], LN_EPS), writes=[self.B_const])

            x3 = self.x32[:, :].rearrange("p (j t) -> p j t", j=16)
            xin = self.d_x.rearrange("j p t -> p j t")
            oo = self.d_out.rearrange("j p t -> p j t")
            finals = []
            stage3 = self.big[:, 0:32 * GT].bitcast(F32).rearrange("p (j t) -> p j t", j=16)

            def load_x(g, staged):
                dst = stage3 if staged else x3
                for q in range(4):
                    wr = ([self.B_big[2 * (4 * q + i) + h] for i in range(4) for h in range(2)] if staged
                          else [self.B_x32[4 * q + i] for i in range(4)])
                    P.op("sp", lambda e, q=q: e.dma_start(out=dst[:, 4 * q:4 * q + 4, :],
                                                          in_=xin[:, 4 * q:4 * q + 4, g * GT:(g + 1) * GT]),
                         writes=wr, dkey="xin%d" % q)
            self.prefetch_next = None
            for gi, g in enumerate(self.groups):
                staged = gi > 0
                if not staged:
                    load_x(g, False)
                if gi + 1 < len(self.groups):
                    gn = self.groups[gi + 1]
                    self.prefetch_next = lambda gn=gn: load_x(gn, True)
                self.c0 = 0
                self.nb = 2
                self.n = GT // 2
                if staged:
                    src = [(self.big[:, 2 * j * GT:(2 * j + 2) * GT].bitcast(F32), [self.B_big[2 * j], self.B_big[2 * j + 1]])
                           for j in range(16)]
                    self.ln_x("ln_in_g", "ln_in_b", src=src)
                else:
                    self.ln_x("ln_in_g", "ln_in_b")
                for l in range(self.depth):
                    self.layer(g, l)
                self.flush_deferred()
                if g == 0:
                    o = P.op("sp", lambda e: e.dma_start(out=oo[:, :, 0:GT - HALO], in_=x3[:, :, HALO:GT]),
                             reads=self.B_x32, dkey="out0")
                else:
                    o = P.op("sp", lambda e: e.dma_start(out=oo[:, :, GT - HALO:OWN], in_=x3[:, :, 0:GT]),
                             reads=self.B_x32, dkey="out1")
                finals.append(o)
            assert self.w_pos == len(self.wseq), (self.w_pos, len(self.wseq))
            P.emit(final_waits=finals)
        return nc


def _fm(v, nt):
    return np.ascontiguousarray(np.asarray(v, np.float32).reshape(nt, 128).T)


def _prep_shared(inp):
    f32 = np.float32
    w_in = np.asarray(inp["w_in"], f32)
    L = DEPTH
    win = np.ascontiguousarray(w_in.reshape(L, 16, 128, 104, 128).transpose(0, 3, 2, 1, 4)).reshape(L, 104, 128, 2048)
    wv = w_in[:, :, 4096:5120]
    wvv = np.ascontiguousarray(wv.reshape(L, 8, 2, 128, 1024).transpose(0, 1, 3, 2, 4)).reshape(L, 8, 128, 2048)
    w_br = np.asarray(inp["w_branch"], f32)
    wbr = np.ascontiguousarray(w_br.reshape(L, 3, 8, 128, 16, 128).transpose(0, 1, 4, 3, 2, 5)).reshape(L, 48, 128, 1024)
    w_out = np.asarray(inp["w_out"], f32)
    wout = np.ascontiguousarray(w_out.reshape(L, 16, 128, 16, 128).transpose(0, 3, 2, 1, 4)).reshape(L, 16, 128, 2048)
    w_fi = np.asarray(inp["w_ffn_in"], f32)
    t = w_fi.reshape(L, 16, 128, 2, 44, 128).transpose(0, 4, 3, 2, 1, 5)
    wfi = np.ascontiguousarray(t).reshape(L, 88, 128, 2048)
    w_fo = np.asarray(inp["w_ffn_out"], f32)
    wfo = np.ascontiguousarray(w_fo.reshape(L, 44, 128, 16, 128).transpose(0, 3, 2, 1, 4)).reshape(L, 16, 128, 5632)
    pp = np.zeros((128, NPP), f32)

    def put(key, arr):
        c = PPM[key]
        pp[:, c:c + arr.shape[1]] = arr
    put("ln_in_g", _fm(inp["ln_in_g"], 16))
    put("ln_in_b", _fm(inp["ln_in_b"], 16))
    for l in range(L):
        put(("gate_bias", l), _fm(np.asarray(inp["gate_bias"])[l], 48))
        ca = np.asarray(inp["conv_a_w"], f32)[l]
        put(("conv_a_w", l), np.ascontiguousarray(ca.reshape(3, 8, 128).transpose(2, 1, 0)).reshape(128, 24))
        put(("sg_ln_g", l), _fm(np.asarray(inp["sg_ln_g"])[l], 8))
        put(("sg_ln_b", l), _fm(np.asarray(inp["sg_ln_b"])[l], 8))
        cw = np.asarray(inp["cc_conv_w"], f32)[l]
        put(("cc_conv_w", l), np.ascontiguousarray(cw.reshape(31, 8, 128).transpose(2, 1, 0)).reshape(128, 248))
        put(("cc_conv_b", l), _fm(np.asarray(inp["cc_conv_b"])[l], 8))
        put(("cc_ln_g", l), _fm(np.asarray(inp["cc_ln_g"])[l], 8))
        put(("cc_ln_b", l), _fm(np.asarray(inp["cc_ln_b"])[l], 8))
        put(("ln_mix_g", l), _fm(np.asarray(inp["ln_mix_g"])[l], 16))
        put(("ln_mix_b", l), _fm(np.asarray(inp["ln_mix_b"])[l], 16))
        put(("ln_ffn_g", l), _fm(np.asarray(inp["ln_ffn_g"])[l], 16))
        put(("ln_ffn_b", l), _fm(np.asarray(inp["ln_ffn_b"])[l], 16))
    sg_w = np.asarray(inp["sg_w"], f32)
    sgwT = np.ascontiguousarray(sg_w.transpose(0, 3, 1, 2)).reshape(L, 128, 1024)
    sgb = np.ascontiguousarray(np.asarray(inp["sg_b"], f32).reshape(L, 1024))
    tri = np.concatenate([np.triu(np.ones((128, 128), f32)), np.eye(128, dtype=f32)], axis=1)
    return {"win": win, "wvv": wvv, "wbr": wbr, "wout": wout, "wfi": wfi, "wfo": wfo, "pp": pp,
            "sgwT": sgwT, "sgb": sgb, "tri": tri}


def _prep_core(x, c):
    b, q = divmod(c, 4)
    t0 = q * OWN
    xs = np.zeros((TL, D_MODEL), np.float32)
    lo = t0 - HALO
    if lo >= 0:
        xs[:] = x[b, lo:t0 + OWN]
        tm = np.ones((128, HALO), np.float32)
    else:
        xs[HALO:] = x[b, 0:OWN]
        tm = np.zeros((128, HALO), np.float32)
    xT = np.ascontiguousarray(xs.T).reshape(16, 128, TL)
    return {"xT": xT, "tmask": tm}


def kernel(**inputs):
    x = np.asarray(inputs["x"], np.float32)
    shared = _prep_shared(inputs)
    in_maps = []
    for c in range(NCORES):
        m = dict(shared)
        m.update(_prep_core(x, c))
        in_maps.append(m)
    nc = KB().build()
    res = run_bass_kernel_spmd(nc, in_maps, core_ids=list(range(NCORES)))
    out = np.empty((2, SEQ, D_MODEL), np.float32)
    for c in range(NCORES):
        b, q = divmod(c, 4)
        o = np.asarray(res.results[c]["out"]).reshape(D_MODEL, OWN)
        out[b, q * OWN:(q + 1) * OWN, :] = o.T
    return out
```

```python
import numpy as np
import concourse.bass as bass
import concourse.mybir as mybir
from concourse.bass_utils import run_bass_kernel_spmd
from contextlib import ExitStack

F32 = mybir.dt.float32
BF16 = mybir.dt.bfloat16
AF = mybir.ActivationFunctionType
ALU = mybir.AluOpType

D_MODEL = 2048
SEQ = 4096
DEPTH = 4
D_FF = 5632
LN_EPS = 1e-5
ALPHA = float((2 * DEPTH) ** 0.25)
NCORES = 8
OWN = 1024
HALO = 256
TL = OWN + HALO
GT = 640
S_L = (64, 96, 128, 224)
S_V = (0, 128, 128, 256)
NS = 6
NSCR = 8
ENGS = ("pe", "act", "dve", "pool", "sp")


class Buf:
    __slots__ = ("name", "writer", "readers")

    def __init__(self, name):
        self.name = name
        self.writer = None
        self.readers = {}


class Op:
    __slots__ = ("eng", "fn", "deps", "sig", "idx", "dkey", "dval")

    def __init__(self, eng, fn):
        self.eng = eng
        self.fn = fn
        self.deps = []
        self.sig = False
        self.idx = 0
        self.dkey = None
        self.dval = 0


class Prog:
    def __init__(self, nc):
        self.nc = nc
        self.q = {e: [] for e in ENGS}
        self.dma_last = {}
        self.dma_cnt = {}

    def op(self, eng, fn, reads=(), writes=(), dkey=None):
        o = Op(eng, fn)
        o.dkey = dkey
        deps = {}

        def add(p, raw):
            if p is None or p is o:
                return
            if p.dkey is None and o.dkey is None and p.eng == eng:
                if eng == "pe" or not raw:
                    return
            deps[id(p)] = p

        for b in reads:
            add(b.writer, True)
        for b in writes:
            add(b.writer, True)
            for r in b.readers.values():
                add(r, False)
        if dkey is not None:
            add(self.dma_last.get(dkey), True)
            self.dma_last[dkey] = o
            self.dma_cnt[dkey] = self.dma_cnt.get(dkey, 0) + 1
            o.dval = 16 * self.dma_cnt[dkey]
        for p in deps.values():
            p.sig = True
        o.deps = list(deps.values())
        for b in writes:
            b.writer = o
            b.readers = {}
        for b in reads:
            b.readers[eng if dkey is None else ("dma", dkey)] = o
        self.q[eng].append(o)
        return o

    def emit(self, final_waits=()):
        nc = self.nc
        for e in ENGS:
            c = 0
            for o in self.q[e]:
                if o.dkey is None and o.sig:
                    c += 1
                    o.idx = c
        with ExitStack() as es:
            semh = {}
            for e in ENGS:
                semh[e] = es.enter_context(nc.semaphore("s_" + e))
            for k in self.dma_cnt:
                semh[("dma", k)] = es.enter_context(nc.semaphore("d_" + str(k)))
            block = es.enter_context(nc.Block())

            def token(p):
                if p.dkey is not None:
                    return ("dma", p.dkey), p.dval
                return p.eng, p.idx

            def body(ename):
                def f(eng):
                    seen = {}
                    for o in self.q[ename]:
                        need = {}
                        for d in o.deps:
                            s, v = token(d)
                            if seen.get(s, 0) < v and need.get(s, 0) < v:
                                need[s] = v
                        for s, v in need.items():
                            eng.wait_ge(semh[s], v)
                            seen[s] = v
                        inst = o.fn(eng)
                        if o.dkey is not None:
                            inst.then_inc(semh[("dma", o.dkey)], 16)
                        elif o.sig:
                            inst.then_inc(semh[ename], 1)
                    if ename == "sp":
                        for p in final_waits:
                            s, v = token(p)
                            eng.wait_ge(semh[s], v)
                return f

            block.tensor(body("pe"))
            block.scalar(body("act"))
            block.vector(body("dve"))
            block.gpsimd(body("pool"))
            block.sync(body("sp"))


def _pp_map():
    m = {}
    c = 0

    def put(name, n):
        nonlocal c
        m[name] = c
        c += n
    put("ln_in_g", 16)
    put("ln_in_b", 16)
    for l in range(DEPTH):
        put(("gate_bias", l), 48)
        put(("conv_a_w", l), 24)
        put(("sg_ln_g", l), 8)
        put(("sg_ln_b", l), 8)
        put(("cc_conv_w", l), 248)
        put(("cc_conv_b", l), 8)
        put(("cc_ln_g", l), 8)
        put(("cc_ln_b", l), 8)
        put(("ln_mix_g", l), 16)
        put(("ln_mix_b", l), 16)
        put(("ln_ffn_g", l), 16)
        put(("ln_ffn_b", l), 16)
    return m, c


PPM, NPP = _pp_map()


class KB:
    def __init__(self, depth=DEPTH, groups=(0, 1)):
        self.depth = depth
        self.groups = groups
        nc = self.nc = bass.Bass("TRN2", target_bir_lowering=False)
        self.P = Prog(nc)
        dt = nc.dram_tensor
        self.d_x = dt("xT", [16, 128, TL], F32, kind="ExternalInput").ap()
        self.d_tmask = dt("tmask", [128, HALO], F32, kind="ExternalInput").ap()
        self.d_win = dt("win", [DEPTH, 104, 128, 2048], F32, kind="ExternalInput").ap()
        self.d_wv = dt("wvv", [DEPTH, 8, 128, 2048], F32, kind="ExternalInput").ap()
        self.d_wbr = dt("wbr", [DEPTH, 48, 128, 1024], F32, kind="ExternalInput").ap()
        self.d_wout = dt("wout", [DEPTH, 16, 128, 2048], F32, kind="ExternalInput").ap()
        self.d_wfi = dt("wfi", [DEPTH, 88, 128, 2048], F32, kind="ExternalInput").ap()
        self.d_wfo = dt("wfo", [DEPTH, 16, 128, 5632], F32, kind="ExternalInput").ap()
        self.d_pp = dt("pp", [128, NPP], F32, kind="ExternalInput").ap()
        self.d_sgw = dt("sgwT", [DEPTH, 128, 1024], F32, kind="ExternalInput").ap()
        self.d_sgb = dt("sgb", [DEPTH, 1024], F32, kind="ExternalInput").ap()
        self.d_tri = dt("tri", [128, 256], F32, kind="ExternalInput").ap()
        self.d_out = dt("out", [16, 128, OWN], F32, kind="ExternalOutput").ap()

    def sb(self, name, shape, dtype):
        return self.es.enter_context(self.nc.sbuf_tensor("sb_" + name, shape, dtype))

    def scr(self):
        i = self.scr_i
        self.scr_i = (i + 1) % NSCR
        return self.scrt[i], self.B_scr[i]

    def newps(self):
        i = self.ps_i
        self.ps_i = (i + 1) % 4
        return self.pst[i], self.B_ps[i]

    def mkviews(self):
        c0, nb, n = self.c0, self.nb, self.n

        def sv(ap2d):
            if nb == 2:
                return ap2d[:, c0:GT].rearrange("p (b n) -> p b n", b=2)
            return ap2d[:, c0:GT]

        def pv(D):
            if nb == 2:
                return D[:, :, 0:n]
            return D[:, 0, 0:n]
        return sv, pv

    def ppc(self, key, col=0):
        c = PPM[key] + col
        return self.pp[:, c:c + 1]

    def bslot(self, s):
        return self.big[:, s * GT:(s + 1) * GT]

    def y_ap(self, br, j):
        return self.bslot(br * 8 + j), [self.B_big[br * 8 + j]]

    def merged_ap(self, j):
        return self.bslot(24 + j), [self.B_big[24 + j]]

    def co_ap(self, j):
        s = 24 + 2 * j
        return self.big[:, s * GT:(s + 2) * GT].bitcast(F32), [self.B_big[s], self.B_big[s + 1]]

    def vn_ap(self, c):
        col = 36 * GT + c * 1024
        s0 = col // GT
        s1 = (col + 1023) // GT
        return self.big[:, col:col + 1024], [self.B_big[s] for s in range(s0, s1 + 1)]

    def h_ap(self, i):
        return self.bslot(i), [self.B_big[i]]

    def wseq_build(self):
        seq = []
        for g in self.groups:
            for l in range(self.depth):
                chunks = list(range((S_V[l] if g == 0 else 0) // 128, GT // 128))
                passes = [chunks[:3], chunks[3:]] if len(chunks) > 3 else [chunks]
                for j in range(8):
                    for e in (40 + j, 48 + j):
                        seq.append((("in", g, l, e), self.d_win[l, e], 2048))
                for pi, pch in enumerate(passes):
                    for u in range(8):
                        seq.append((("v", g, l, pi, u), self.d_wv[l, u], 2048))
                for j in range(8):
                    for e in (8 + j, 16 + j, j):
                        seq.append((("in", g, l, e), self.d_win[l, e], 2048))
                for gg in range(8):
                    seq.append((("in", g, l, 24 + gg), self.d_win[l, 24 + gg], 2048))
                for j in range(16):
                    for n in range(3):
                        e = 56 + n * 16 + j
                        seq.append((("in", g, l, e), self.d_win[l, e], 2048))
                        seq.append((("br", g, l, n, j), self.d_wbr[l, n * 16 + j], 1024))
                for j in range(16):
                    seq.append((("out", g, l, j), self.d_wout[l, j], 2048))
                for i in range(44):
                    seq.append((("fi", g, l, 2 * i), self.d_wfi[l, 2 * i], 2048))
                    seq.append((("fi", g, l, 2 * i + 1), self.d_wfi[l, 2 * i + 1], 2048))
                for j in range(16):
                    for (k0, nk) in ((0, 16), (16, 16), (32, 12)):
                        seq.append((("fo", g, l, j, k0), self.d_wfo[l, j][:, k0 * 128:(k0 + nk) * 128], nk * 128))
        self.wseq = seq
        self.w_issued = 0
        self.w_pos = 0

    def wget(self, key, hold=False):
        seq = self.wseq
        assert seq[self.w_pos][0] == key, (seq[self.w_pos][0], key)
        while self.w_issued < min(len(seq), self.w_pos + (1 if hold else NS)):
            i = self.w_issued
            _, src, ncols = seq[i]
            s = i % NS
            self.P.op("pool", lambda e, s=s, src=src, ncols=ncols: e.dma_start(out=self.ring[s][:, 0:ncols], in_=src),
                      writes=[self.B_ring[s]], dkey="w%d" % s)
            self.w_issued += 1
        s = self.w_pos % NS
        self.w_pos += 1
        return self.ring[s], self.B_ring[s]

    def mm_fm(self, D, BD, wt, Bw, nk, act3, Bact, kbase=0, start=True, stop=True):
        c0, n, nb = self.c0, self.n, self.nb

        def f(e):
            r = None
            for b in range(nb):
                for k in range(nk):
                    r = e.matmul(D[:, b, 0:n], lhsT=wt[:, k * 128:(k + 1) * 128],
                                 rhs=act3[:, kbase + k, c0 + b * n:c0 + (b + 1) * n],
                                 start=(start and k == 0), stop=(stop and k == nk - 1))
            return r
        self.P.op("pe", f, reads=[Bw] + list(Bact), writes=[BD])

    def kouter(self, keys):
        P = self.P
        c0, n, nb = self.c0, self.n, self.nb
        xh3 = self.xh[:, :].rearrange("p (j t) -> p j t", j=16)
        units = []
        for i, key in enumerate(keys):
            wt, Bw = self.wget(key, hold=(i > 0))
            D, BD = self.newps()
            units.append((key, wt, Bw, D, BD))
        for k in range(16):
            def f(e, k=k):
                r = None
                for (_, wt, _, D, _) in units:
                    for b in range(nb):
                        r = e.matmul(D[:, b, 0:n], lhsT=wt[:, k * 128:(k + 1) * 128],
                                     rhs=xh3[:, k, c0 + b * n:c0 + (b + 1) * n], start=(k == 0), stop=(k == 15))
                return r
            P.op("pe", f, reads=[self.B_xh[k]] + [u[2] for u in units], writes=[u[4] for u in units])
        for u in units:
            self.pre[u[0]] = (u[3], u[4])

    def unit_xh(self, key):
        if key in self.pre:
            return self.pre.pop(key)
        wt, Bw = self.wget(key)
        D, BD = self.newps()
        xh3 = self.xh[:, :].rearrange("p (j t) -> p j t", j=16)
        self.mm_fm(D, BD, wt, Bw, 16, xh3, self.B_xh)
        return D, BD

    def st_begin(self, pool_sum=False):
        self.st_n = 0
        self.st_eng = "pool" if pool_sum else "dve"

    def st_add(self, tap, tb):
        P = self.P
        c0 = self.c0
        accs, Bas = self.lnt[0], self.B_lnt[0]
        accq, Baq = self.lnt[1], self.B_lnt[1]
        j = self.st_n
        self.st_n += 1
        if j == 0:
            self.st_first = (tap, tb)
            P.op("act", lambda e: e.activation(out=accq[:, c0:GT], in_=tap[:, c0:GT], func=AF.Square),
                 reads=list(tb), writes=[Baq])
            return
        se = self.st_eng
        if j == 1:
            t0ap, t0b = self.st_first
            P.op(se, lambda e: e.tensor_tensor(out=accs[:, c0:GT], in0=t0ap[:, c0:GT], in1=tap[:, c0:GT], op=ALU.add),
                 reads=list(tb) + list(t0b), writes=[Bas])
        else:
            P.op(se, lambda e: e.tensor_tensor(out=accs[:, c0:GT], in0=accs[:, c0:GT], in1=tap[:, c0:GT], op=ALU.add),
                 reads=list(tb) + [Bas], writes=[Bas])
        sq, Bsq = self.scr()
        P.op("act", lambda e: e.activation(out=sq[:, c0:GT], in_=tap[:, c0:GT], func=AF.Square),
             reads=list(tb), writes=[Bsq])
        P.op("dve", lambda e: e.tensor_tensor(out=accq[:, c0:GT], in0=accq[:, c0:GT], in1=sq[:, c0:GT], op=ALU.add),
             reads=[Bsq, Baq], writes=[Baq])

    def st_finish(self, nfeat):
        P = self.P
        c0, n, nb = self.c0, self.n, self.nb
        sv, pv = self.mkviews()
        accs, Bas = self.lnt[0], self.B_lnt[0]
        accq, Baq = self.lnt[1], self.B_lnt[1]
        DS, BS = self.newps()
        DQ, BQ = self.newps()

        def fs(e):
            r = None
            for b in range(nb):
                r = e.matmul(DS[:, b, 0:n], lhsT=self.ones32[:, :], rhs=accs[:, c0 + b * n:c0 + (b + 1) * n], start=True, stop=True)
            return r
        P.op("pe", fs, reads=[Bas, self.B_const], writes=[BS])

        def fq(e):
            r = None
            for b in range(nb):
                r = e.matmul(DQ[:, b, 0:n], lhsT=self.ones32[:, :], rhs=accq[:, c0 + b * n:c0 + (b + 1) * n], start=True, stop=True)
            return r
        P.op("pe", fq, reads=[Baq, self.B_const], writes=[BQ])
        inv = 1.0 / nfeat
        mean, Bm = self.lnt[0], self.B_lnt[0]
        rstd, Br = self.lnt[1], self.B_lnt[1]
        t1, Bt1 = self.lnt[2], self.B_lnt[2]
        P.op("dve", lambda e: e.tensor_scalar(out=sv(mean), in0=pv(DS), scalar1=inv, scalar2=None, op0=ALU.mult),
             reads=[BS], writes=[Bm])
        P.op("dve", lambda e: e.tensor_tensor(out=t1[:, c0:GT], in0=mean[:, c0:GT], in1=mean[:, c0:GT], op=ALU.mult),
             reads=[Bm], writes=[Bt1])
        P.op("dve", lambda e: e.scalar_tensor_tensor(out=sv(rstd), in0=pv(DQ), scalar=inv, in1=sv(t1),
                                                     op0=ALU.mult, op1=ALU.subtract),
             reads=[BQ, Bt1], writes=[Br])
        P.op("act", lambda e: e.activation(out=t1[:, c0:GT], in_=rstd[:, c0:GT], func=AF.Sqrt, bias=self.epsc[:, 0:1], scale=1.0),
             reads=[Br, self.B_const], writes=[Bt1])

        def recip():
            P.op("dve", lambda e: e.reciprocal(out=rstd[:, c0:GT], in_=t1[:, c0:GT]),
                 reads=[Bt1], writes=[Br])
        self.ln_mean = (mean, Bm)
        return DS, BS, -inv, rstd, Br, recip

    def ln_apply(self, tiles, gkey, nfeat, outs):
        P = self.P
        c0 = self.c0
        sv, pv = self.mkviews()
        DS, BS, ninv, rstd, Br, recip = self.st_finish(nfeat)
        mean, Bm = self.ln_mean
        nt = len(tiles)
        for b0 in range(0, nt, 8):
            ds = []
            for j in range(b0, min(nt, b0 + 8)):
                tap, tb = tiles[j]
                d, Bd = self.scr()
                ds.append((j, d, Bd))
                P.op("dve", lambda e, tap=tap, d=d: e.tensor_tensor(out=d[:, c0:GT], in0=tap[:, c0:GT], in1=mean[:, c0:GT], op=ALU.subtract),
                     reads=list(tb) + [Bm], writes=[Bd])
            if b0 == 0:
                recip()
            for (j, d, Bd) in ds:
                P.op("dve", lambda e, d=d, j=j: e.scalar_tensor_tensor(out=d[:, c0:GT], in0=d[:, c0:GT], scalar=self.ppc(gkey, j),
                                                                     in1=rstd[:, c0:GT], op0=ALU.mult, op1=ALU.mult),
                     reads=[Bd, Br, self.B_pp], writes=[Bd])
                outs(j, d, Bd)

    def ln_x(self, gkey, bkey, accumulated=False, final=False, src=None):
        P = self.P
        c0 = self.c0
        sv, pv = self.mkviews()
        if src is None:
            src = [(self.x32[:, j * GT:(j + 1) * GT], [self.B_x32[j]]) for j in range(16)]
        if not accumulated:
            self.st_begin(pool_sum=True)
            for j in range(16):
                self.st_add(*src[j])
        DS, BS, ninv, rstd, Br, recip = self.st_finish(D_MODEL)
        mean, Bm = self.ln_mean
        for j in range(4, 16):
            xj = self.x32[:, j * GT:(j + 1) * GT]
            sj, sb_ = src[j]
            P.op("pool", lambda e, xj=xj, sj=sj: e.tensor_tensor(out=xj[:, c0:GT], in0=sj[:, c0:GT], in1=mean[:, c0:GT], op=ALU.subtract),
                 reads=list(sb_) + [Bm], writes=[self.B_x32[j]])
        for j in range(4):
            xj = self.x32[:, j * GT:(j + 1) * GT]
            sj, sb_ = src[j]
            P.op("dve", lambda e, xj=xj, sj=sj: e.scalar_tensor_tensor(out=sv(xj), in0=pv(DS), scalar=ninv, in1=sv(sj),
                                                                      op0=ALU.mult, op1=ALU.add),
                 reads=list(sb_) + [BS], writes=[self.B_x32[j]])
            if j == 2:
                recip()
        for j in range(16):
            xj = self.x32[:, j * GT:(j + 1) * GT]
            hj = self.xh[:, j * GT:(j + 1) * GT]
            P.op("dve", lambda e, xj=xj, j=j: e.scalar_tensor_tensor(out=xj[:, c0:GT], in0=xj[:, c0:GT], scalar=self.ppc(gkey, j),
                                                                   in1=rstd[:, c0:GT], op0=ALU.mult, op1=ALU.mult),
                 reads=[self.B_x32[j], Br, self.B_pp], writes=[self.B_x32[j]])
            if final:
                P.op("act", lambda e, xj=xj, j=j: e.activation(out=xj[:, c0:GT], in_=xj[:, c0:GT], func=AF.Identity,
                                                             bias=self.ppc(bkey, j), scale=1.0),
                     reads=[self.B_x32[j], self.B_pp], writes=[self.B_x32[j]])
                continue
            P.op("act", lambda e, xj=xj, hj=hj, j=j: e.activation(out=hj[:, c0:GT], in_=xj[:, c0:GT], func=AF.Identity,
                                                                 bias=self.ppc(bkey, j), scale=1.0),
                 reads=[self.B_x32[j], self.B_pp], writes=[self.B_xh[j]])

            def late(xj=xj, j=j):
                P.op("act", lambda e: e.activation(out=xj[:, c0:GT], in_=xj[:, c0:GT], func=AF.Identity,
                                                   bias=self.ppc(bkey, j), scale=1.0),
                     reads=[self.B_x32[j], self.B_pp], writes=[self.B_x32[j]])
            self.deferred.append(late)

    def run_deferred(self, k=1):
        while k > 0 and self.deferred:
            self.deferred.pop(0)()
            k -= 1

    def flush_deferred(self):
        self.run_deferred(len(self.deferred))

    def layer(self, g, l):
        P = self.P
        c0 = self.c0 = (S_L[l] if g == 0 else 0)
        Tg = GT - c0
        self.nb = nb = 2 if Tg > 512 else 1
        self.n = n = Tg // nb
        sv, pv = self.mkviews()
        xh3 = self.xh[:, :].rearrange("p (j t) -> p j t", j=16)
        big3 = self.big[:, :].rearrange("p (j t) -> p j t", j=44)
        nmask = max(0, HALO - c0) if g == 0 else 0

        P.op("pool", lambda e: e.dma_start(out=self.wTm[:, :], in_=self.d_sgw[l]), writes=[self.B_wTm], dkey="sgw")
        P.op("dve", lambda e: e.tensor_tensor(out=self.wTm[:, :].rearrange("p (g t) -> p g t", g=8),
                                              in0=self.wTm[:, :].rearrange("p (g t) -> p g t", g=8),
                                              in1=self.tri[:, 0:128].unsqueeze(1).to_broadcast([128, 8, 128]), op=ALU.mult),
             reads=[self.B_wTm, self.B_const], writes=[self.B_wTm])
        P.op("sp", lambda e: e.dma_start(out=self.Cc[:, :], in_=self.d_sgb[l].partition_broadcast(128)),
             writes=[self.B_Cc], dkey="sgb")
        def c_stage1(j):
            DA, BA = self.unit_xh(("in", g, l, 40 + j))
            DG, BG = self.unit_xh(("in", g, l, 48 + j))
            s1, Bs1 = self.scr()
            P.op("act", lambda e, s1=s1, DG=DG: e.activation(out=sv(s1), in_=pv(DG), func=AF.Sigmoid), reads=[BG], writes=[Bs1])
            self.run_deferred(2)
            ci = self.cv_i
            self.cv_i = (ci + 1) % 2
            yp, Byp = self.ygb[ci], self.B_ygb[ci]
            dg, Bdg = self.diag[ci], self.B_diag[ci]
            yv = yp[:, 30:30 + GT]
            tb = (l * 8 + j) * 30
            wc = PPM[("cc_conv_w", l)] + j * 31
            P.op("dve", lambda e, dg=dg, wc=wc: e.tensor_tensor(
                out=dg[:, :].rearrange("p (k m) -> p k m", k=31),
                in0=self.tri[:, 128:256].unsqueeze(1).to_broadcast([128, 31, 128]),
                in1=self.pp[:, wc:wc + 31].unsqueeze(2).to_broadcast([128, 31, 128]), op=ALU.mult),
                reads=[self.B_const, self.B_pp], writes=[Bdg])
            if g == 0:
                P.op("dve", lambda e, yp=yp: e.memset(yp[:, c0:c0 + 30], 0.0), writes=[Byp])
            else:
                P.op("dve", lambda e, yp=yp, tb=tb: e.tensor_copy(out=yp[:, 0:30], in_=self.ctail[:, tb:tb + 30]),
                     reads=[self.B_ctail[l * 8 + j]], writes=[Byp])
            P.op("dve", lambda e, yv=yv, s1=s1, DA=DA: e.tensor_tensor(out=sv(yv), in0=sv(s1), in1=pv(DA), op=ALU.mult),
                 reads=[Bs1, BA], writes=[Byp])
            if nmask:
                P.op("dve", lambda e, yv=yv: e.tensor_tensor(out=yv[:, c0:HALO], in0=yv[:, c0:HALO], in1=self.tmask[:, c0:HALO], op=ALU.mult),
                     reads=[Byp, self.B_const], writes=[Byp])
            if g == 0:
                P.op("dve", lambda e, yp=yp, tb=tb: e.tensor_copy(out=self.ctail[:, tb:tb + 30], in_=yp[:, GT:GT + 30]),
                     reads=[Byp], writes=[self.B_ctail[l * 8 + j]])
            return (j, yp, Byp, dg, Bdg)

        def c_stage2(ctx):
            j, yp, Byp, dg, Bdg = ctx
            DV, BV = self.newps()

            def fcv(e, DV=DV, dg=dg, yp=yp):
                r = None
                for b in range(nb):
                    for k in range(31):
                        r = e.matmul(DV[:, b, 0:n], lhsT=dg[:, k * 128:(k + 1) * 128],
                                     rhs=yp[:, c0 + b * n + k:c0 + b * n + k + n], start=(k == 0), stop=(k == 30))
                return r
            P.op("pe", fcv, reads=[Bdg, Byp], writes=[BV])
            co, Bco = self.co_ap(j)
            P.op("act", lambda e, DV=DV, co=co, j=j: e.activation(out=sv(co), in_=pv(DV), func=AF.Identity,
                                                                 bias=self.ppc(("cc_conv_b", l), j), scale=1.0),
                 reads=[BV, self.B_pp], writes=Bco)
            if j == 0:
                self.st_begin()
            self.st_add(co, Bco)

        self.kouter([("in", g, l, 40), ("in", g, l, 48), ("in", g, l, 41)])
        pend = None
        for j in range(8):
            ctx = c_stage1(j)
            if pend is not None:
                c_stage2(pend)
            pend = ctx
        c_stage2(pend)
        DR, BR = self.newps()

        def frs(e):
            r = None
            for b in range(2):
                r = e.matmul(DR[:, b, :], lhsT=self.ones16[:, :], rhs=self.wTm[:, b * 512:(b + 1) * 512], start=True, stop=True)
            return r
        P.op("pe", frs, reads=[self.B_wTm, self.B_const], writes=[BR])
        for gg in range(8):
            P.op("dve", lambda e, gg=gg: e.scalar_tensor_tensor(
                out=self.Cc[:, gg * 128:(gg + 1) * 128], in0=DR[:, gg // 4, (gg % 4) * 128:(gg % 4 + 1) * 128],
                scalar=self.ppc(("sg_ln_b", l), gg), in1=self.Cc[:, gg * 128:(gg + 1) * 128], op0=ALU.mult, op1=ALU.add),
                reads=[BR, self.B_pp, self.B_Cc], writes=[self.B_Cc])

        tiles = [self.co_ap(j) for j in range(8)]

        def outs_c(j, d, Bd):
            yc, Byc = self.y_ap(2, j)
            P.op("act", lambda e: e.activation(out=yc[:, c0:GT], in_=d[:, c0:GT], func=AF.Silu,
                                               bias=self.ppc(("cc_ln_b", l), j), scale=1.0),
                 reads=[Bd, self.B_pp], writes=Byc)
        self.ln_apply(tiles, ("cc_ln_g", l), 1024, outs_c)

        chunks = list(range((S_V[l] if g == 0 else 0) // 128, GT // 128))
        passes = [chunks[:3], chunks[3:]] if len(chunks) > 3 else [chunks]
        for pi, pch in enumerate(passes):
            Ds = [self.newps() for _ in pch]
            for u in range(8):
                wt, Bw = self.wget(("v", g, l, pi, u))
                for ci, c in enumerate(pch):
                    D, BD = Ds[ci]

                    def fv(e, wt=wt, D=D, c=c, u=u):
                        r = None
                        for kk in range(2):
                            k = 2 * u + kk
                            for cb in range(2):
                                r = e.matmul(D[:, cb, :], lhsT=xh3[:, k, c * 128:(c + 1) * 128],
                                             rhs=wt[:, kk * 1024 + cb * 512:kk * 1024 + (cb + 1) * 512],
                                             start=(k == 0), stop=(k == 15))
                        return r
                    P.op("pe", fv, reads=[Bw, self.B_xh[2 * u], self.B_xh[2 * u + 1]], writes=[BD])
            for ci, c in enumerate(pch):
                D, BD = Ds[ci]
                si = self.vs_i
                self.vs_i = (si + 1) % 2
                st = self.vst[si]
                Bst = self.B_vst[si]
                P.op("dve", lambda e, D=D, st=st: e.bn_stats(out=st[:, 0:6], in_=D[:, 0, :]), reads=[BD], writes=[Bst])
                P.op("dve", lambda e, D=D, st=st: e.bn_stats(out=st[:, 6:12], in_=D[:, 1, :]), reads=[BD], writes=[Bst])
                P.op("dve", lambda e, st=st: e.bn_aggr(out=st[:, 12:14], in_=st[:, 0:12]), reads=[Bst], writes=[Bst])
                P.op("dve", lambda e, st=st: e.tensor_scalar(out=st[:, 14:15], in0=st[:, 13:14], scalar1=LN_EPS, scalar2=None, op0=ALU.add),
                     reads=[Bst], writes=[Bst])
                P.op("act", lambda e, st=st: e.activation(out=st[:, 15:16], in_=st[:, 14:15], func=AF.Sqrt), reads=[Bst], writes=[Bst])
                P.op("dve", lambda e, st=st: e.reciprocal(out=st[:, 16:17], in_=st[:, 15:16]), reads=[Bst], writes=[Bst])
                vn, Bvn = self.vn_ap(c)
                P.op("dve", lambda e, D=D, st=st, vn=vn: e.tensor_scalar(
                    out=vn.rearrange("p (b n) -> p b n", b=2), in0=D[:, :, :], scalar1=st[:, 12:13], scalar2=st[:, 16:17],
                    op0=ALU.subtract, op1=ALU.mult), reads=[BD, Bst], writes=Bvn)

        for j in range(8):
            wt, Bw = self.wget(("in", g, l, 8 + j))
            DC, BC = self.newps()
            self.mm_fm(DC, BC, wt, Bw, 16, xh3, self.B_xh)
            wt, Bw = self.wget(("in", g, l, 16 + j))
            DH, BH = self.newps()
            self.mm_fm(DH, BH, wt, Bw, 16, xh3, self.B_xh)
            wt, Bw = self.wget(("in", g, l, j))
            DB, BB = self.newps()
            self.mm_fm(DB, BB, wt, Bw, 16, xh3, self.B_xh)
            s1, Bs1 = self.scr()
            P.op("act", lambda e, s1=s1, DC=DC: e.activation(out=sv(s1), in_=pv(DC), func=AF.Copy), reads=[BC], writes=[Bs1])
            pp_, Bpp_ = self.scr()
            pv2 = pp_[:, 2:2 + GT]
            if g == 0:
                P.op("dve", lambda e, pp_=pp_: e.memset(pp_[:, c0:c0 + 2], 0.0), writes=[Bpp_])
            else:
                P.op("dve", lambda e, pp_=pp_, j=j: e.tensor_copy(out=pp_[:, 0:2], in_=self.atail[:, (l * 8 + j) * 2:(l * 8 + j) * 2 + 2]),
                     reads=[self.B_atail[l * 8 + j]], writes=[Bpp_])
            P.op("dve", lambda e, pv2=pv2, s1=s1, DH=DH: e.tensor_tensor(out=sv(pv2), in0=sv(s1), in1=pv(DH), op=ALU.mult),
                 reads=[Bs1, BH], writes=[Bpp_])
            if nmask:
                P.op("dve", lambda e, pv2=pv2: e.tensor_tensor(out=pv2[:, c0:HALO], in0=pv2[:, c0:HALO], in1=self.tmask[:, c0:HALO], op=ALU.mult),
                     reads=[Bpp_, self.B_const], writes=[Bpp_])
            if g == 0:
                P.op("dve", lambda e, pp_=pp_, j=j: e.tensor_copy(out=self.atail[:, (l * 8 + j) * 2:(l * 8 + j) * 2 + 2], in_=pp_[:, GT:GT + 2]),
                     reads=[Bpp_], writes=[self.B_atail[l * 8 + j]])
            acc, Bacc = self.scr()
            wc = PPM[("conv_a_w", l)] + j * 3
            P.op("dve", lambda e, acc=acc, pp_=pp_, wc=wc: e.tensor_scalar(out=acc[:, c0:GT], in0=pp_[:, c0:GT], scalar1=self.pp[:, wc:wc + 1],
                                                                          scalar2=None, op0=ALU.mult),
                 reads=[Bpp_, self.B_pp], writes=[Bacc])
            for k in (1, 2):
                P.op("dve", lambda e, acc=acc, pp_=pp_, wc=wc, k=k: e.scalar_tensor_tensor(
                    out=acc[:, c0:GT], in0=pp_[:, c0 + k:GT + k], scalar=self.pp[:, wc + k:wc + k + 1], in1=acc[:, c0:GT],
                    op0=ALU.mult, op1=ALU.add), reads=[Bpp_, Bacc, self.B_pp], writes=[Bacc])
            ya, Bya = self.y_ap(0, j)
            P.op("dve", lambda e, acc=acc, DB=DB, ya=ya: e.tensor_tensor(out=sv(ya), in0=sv(acc), in1=pv(DB), op=ALU.mult),
                 reads=[Bacc, BB], writes=Bya)

        for gg in range(8):
            wt, Bw = self.wget(("in", g, l, 24 + gg))
            DU, BU = self.newps()
            self.mm_fm(DU, BU, wt, Bw, 16, xh3, self.B_xh)
            DM, BM = self.newps()

            def fm(e, gg=gg, DM=DM):
                r = None
                for c in chunks:
                    a = c * 128 - c0
                    bnd = [max(a, 0), a + 128]
                    if nb == 2 and bnd[0] < n < a + 128:
                        bnd = [bnd[0], n, a + 128]
                    for i in range(len(bnd) - 1):
                        lo, hi = bnd[i], bnd[i + 1]
                        b = lo // n
                        vn, _ = self.vn_ap(c)
                        r = e.matmul(DM[:, b, lo - b * n:hi - b * n], lhsT=vn[:, gg * 128:(gg + 1) * 128],
                                     rhs=self.wTm[:, gg * 128 + (lo - a):gg * 128 + (hi - a)], start=True, stop=True,
                                     skip_group_check=True)
                return r
            rd = [self.B_wTm]
            for c in chunks:
                rd += self.vn_ap(c)[1]
            P.op("pe", fm, reads=rd, writes=[BM])
            t, Bt = self.scr()
            nch = len(chunks)
            if nb == 1 or True:
                for b in range(nb):
                    lo = c0 + b * n
                    P.op("dve", lambda e, gg=gg, DM=DM, t=t, b=b, lo=lo: e.tensor_scalar(
                        out=t[:, lo:lo + n], in0=DM[:, b, 0:n], scalar1=self.ppc(("sg_ln_g", l), gg), scalar2=None, op0=ALU.mult),
                        reads=[BM, self.B_pp], writes=[Bt])
            cs0 = ((c0 + 127) // 128) * 128
            nch = (GT - cs0) // 128
            if cs0 > c0 and chunks[0] * 128 < cs0:
                P.op("dve", lambda e, gg=gg, t=t, cs0=cs0: e.tensor_tensor(
                    out=t[:, c0:cs0], in0=t[:, c0:cs0],
                    in1=self.Cc[:, gg * 128 + 128 - (cs0 - c0):(gg + 1) * 128], op=ALU.add),
                    reads=[Bt, self.B_Cc], writes=[Bt])
            P.op("dve", lambda e, gg=gg, t=t, cs0=cs0: e.tensor_tensor(
                out=t[:, cs0:GT].rearrange("p (c t) -> p c t", t=128), in0=t[:, cs0:GT].rearrange("p (c t) -> p c t", t=128),
                in1=self.Cc[:, gg * 128:(gg + 1) * 128].unsqueeze(1).to_broadcast([128, nch, 128]), op=ALU.add),
                reads=[Bt, self.B_Cc], writes=[Bt])
            yb, Byb = self.y_ap(1, gg)
            P.op("dve", lambda e, t=t, DU=DU, yb=yb: e.tensor_tensor(out=sv(yb), in0=sv(t), in1=pv(DU), op=ALU.mult),
                 reads=[Bt, BU], writes=Byb)

        for j in range(16):
            m, Bm_ = self.scr()
            mg, Bmg = self.merged_ap(j)
            for nbr in range(3):
                wt, Bw = self.wget(("in", g, l, 56 + nbr * 16 + j))
                DG, BG = self.newps()
                self.mm_fm(DG, BG, wt, Bw, 16, xh3, self.B_xh)
                wt, Bw = self.wget(("br", g, l, nbr, j))
                DP, BP = self.newps()
                self.mm_fm(DP, BP, wt, Bw, 8, big3, [self.B_big[nbr * 8 + k] for k in range(8)], kbase=nbr * 8)
                gs, Bgs = self.scr()
                P.op("act", lambda e, gs=gs, DG=DG, nbr=nbr, j=j: e.activation(out=sv(gs), in_=pv(DG), func=AF.Sigmoid,
                                                                             bias=self.ppc(("gate_bias", l), nbr * 16 + j), scale=1.0),
                     reads=[BG, self.B_pp], writes=[Bgs])
                if nbr == 0:
                    P.op("dve", lambda e, m=m, gs=gs, DP=DP: e.tensor_tensor(out=sv(m), in0=sv(gs), in1=pv(DP), op=ALU.mult),
                         reads=[Bgs, BP], writes=[Bm_])
                else:
                    P.op("dve", lambda e, gs=gs, DP=DP: e.tensor_tensor(out=sv(gs), in0=sv(gs), in1=pv(DP), op=ALU.mult),
                         reads=[Bgs, BP], writes=[Bgs])
                    if nbr == 1:
                        P.op("dve", lambda e, m=m, gs=gs: e.tensor_tensor(out=m[:, c0:GT], in0=m[:, c0:GT], in1=gs[:, c0:GT], op=ALU.add),
                             reads=[Bm_, Bgs], writes=[Bm_])
                    else:
                        P.op("dve", lambda e, m=m, gs=gs, mg=mg: e.tensor_tensor(out=mg[:, c0:GT], in0=m[:, c0:GT], in1=gs[:, c0:GT], op=ALU.add),
                             reads=[Bm_, Bgs], writes=Bmg)

        self.flush_deferred()
        for j in range(16):
            wt, Bw = self.wget(("out", g, l, j))
            DO, BO = self.newps()
            self.mm_fm(DO, BO, wt, Bw, 16, big3, [self.B_big[24 + k] for k in range(16)], kbase=24)
            xj = self.x32[:, j * GT:(j + 1) * GT]
            P.op("dve", lambda e, xj=xj, DO=DO: e.scalar_tensor_tensor(out=sv(xj), in0=sv(xj), scalar=ALPHA, in1=pv(DO),
                                                                      op0=ALU.mult, op1=ALU.add),
                 reads=[self.B_x32[j], BO], writes=[self.B_x32[j]])
            if j == 0:
                self.st_begin()
            self.st_add(xj, [self.B_x32[j]])
        self.ln_x(("ln_mix_g", l), ("ln_mix_b", l), accumulated=True)

        self.kouter([("fi", g, l, 0), ("fi", g, l, 1), ("fi", g, l, 2)])
        for i in range(44):
            DG, BG = self.unit_xh(("fi", g, l, 2 * i))
            DU, BU = self.unit_xh(("fi", g, l, 2 * i + 1))
            s1, Bs1 = self.scr()
            P.op("act", lambda e, s1=s1, DG=DG: e.activation(out=sv(s1), in_=pv(DG), func=AF.Silu), reads=[BG], writes=[Bs1])
            self.run_deferred(1)
            hi, Bhi = self.h_ap(i)
            P.op("dve", lambda e, s1=s1, DU=DU, hi=hi: e.tensor_tensor(out=sv(hi), in0=sv(s1), in1=pv(DU), op=ALU.mult),
                 reads=[Bs1, BU], writes=Bhi)
        self.flush_deferred()
        for j in range(16):
            DF, BF = self.newps()
            for (k0, nk) in ((0, 16), (16, 16), (32, 12)):
                wt, Bw = self.wget(("fo", g, l, j, k0))
                self.mm_fm(DF, BF, wt, Bw, nk, big3, [self.B_big[k0 + k] for k in range(nk)], kbase=k0,
                           start=(k0 == 0), stop=(k0 == 32))
            xj = self.x32[:, j * GT:(j + 1) * GT]
            P.op("dve", lambda e, xj=xj, DF=DF: e.scalar_tensor_tensor(out=sv(xj), in0=sv(xj), scalar=ALPHA, in1=pv(DF),
                                                                      op0=ALU.mult, op1=ALU.add),
                 reads=[self.B_x32[j], BF], writes=[self.B_x32[j]])
            if j == 0:
                self.st_begin()
            self.st_add(xj, [self.B_x32[j]])
        if l == self.depth - 1 and self.prefetch_next is not None:
            self.prefetch_next()
            self.prefetch_next = None
        self.ln_x(("ln_ffn_g", l), ("ln_ffn_b", l), accumulated=True, final=(l == self.depth - 1))

    def build(self):
        nc = self.nc
        P = self.P
        with ExitStack() as es:
            self.es = es
            self.x32 = self.sb("x32", [128, 16 * GT], F32)
            self.xh = self.sb("xh", [128, 16 * GT], BF16)
            self.big = self.sb("big", [128, 44 * GT], BF16)
            self.ring = [self.sb("ring%d" % i, [128, 2048], BF16) for i in range(NS)]
            self.scrt = [self.sb("scr%d" % i, [128, 704], F32) for i in range(NSCR)]
            self.ygb = [self.sb("ygb%d" % i, [128, 30 + GT + 2], BF16) for i in range(2)]
            self.diag = [self.sb("diag%d" % i, [128, 31 * 128], BF16) for i in range(2)]
            self.B_ygb = [Buf("ygb%d" % i) for i in range(2)]
            self.B_diag = [Buf("diag%d" % i) for i in range(2)]
            self.cv_i = 0
            self.pp = self.sb("pp", [128, NPP], F32)
            self.ones32 = self.sb("ones32", [128, 128], F32)
            self.ones16 = self.sb("ones16", [128, 128], BF16)
            self.epsc = self.sb("epsc", [128, 2], F32)
            self.tri = self.sb("tri", [128, 256], F32)
            self.tmask = self.sb("tmask", [128, HALO], F32)
            self.wTm = self.sb("wTm", [128, 1024], BF16)
            self.Cc = self.sb("Cc", [128, 1024], F32)
            self.ctail = self.sb("ctail", [128, DEPTH * 8 * 30], BF16)
            self.atail = self.sb("atail", [128, DEPTH * 8 * 2], F32)
            self.vst = [self.sb("vst%d" % i, [128, 24], F32) for i in range(2)]
            self.lnt = [self.sb("lnt%d" % i, [128, GT], F32) for i in range(3)]
            self.B_lnt = [Buf("lnt%d" % i) for i in range(3)]
            self.pst = [es.enter_context(nc.psum_tensor("ps%d" % i, [128, 2, 512], F32)) for i in range(4)]
            self.B_x32 = [Buf("x32_%d" % j) for j in range(16)]
            self.B_xh = [Buf("xh_%d" % j) for j in range(16)]
            self.B_big = [Buf("big_%d" % j) for j in range(44)]
            self.B_ring = [Buf("ring_%d" % j) for j in range(NS)]
            self.B_scr = [Buf("scr_%d" % j) for j in range(NSCR)]
            self.B_ps = [Buf("ps_%d" % j) for j in range(4)]
            self.B_pp = Buf("pp")
            self.B_const = Buf("const")
            self.B_wTm = Buf("wTm")
            self.B_Cc = Buf("Cc")
            self.B_ctail = [Buf("ct%d" % j) for j in range(DEPTH * 8)]
            self.B_atail = [Buf("at%d" % j) for j in range(DEPTH * 8)]
            self.B_vst = [Buf("vst%d" % j) for j in range(2)]
            self.deferred = []
            self.pre = {}
            self.scr_i = 0
            self.ps_i = 0
            self.vs_i = 0
            self.wseq_build()

            P.op("sp", lambda e: e.dma_start(out=self.pp[:, :], in_=self.d_pp), writes=[self.B_pp], dkey="pp")
            P.op("sp", lambda e: e.dma_start(out=self.tri[:, :], in_=self.d_tri), writes=[self.B_const], dkey="c0")
            P.op("sp", lambda e: e.dma_start(out=self.tmask[:, :], in_=self.d_tmask), writes=[self.B_const], dkey="c0")
            P.op("dve", lambda e: e.memset(self.ones32[:, :], 1.0), writes=[self.B_const])
            P.op("dve", lambda e: e.memset(self.ones16[:, :], 1.0), writes=[self.B_const])
            P.op("dve", lambda e: e.memset(self.epsc[:, :], LN_EPS), writes=[self.B_const])

            x3 = self.x32[:, :].rearrange("p (j t) -> p j t", j=16)
            xin = self.d_x.rearrange("j p t -> p j t")
            oo = self.d_out.rearrange("j p t -> p j t")
            finals = []
            stage3 = self.big[:, 0:32 * GT].bitcast(F32).rearrange("p (j t) -> p j t", j=16)

            def load_x(g, staged):
                dst = stage3 if staged else x3
                for q in range(4):
                    wr = ([self.B_big[2 * (4 * q + i) + h] for i in range(4) for h in range(2)] if staged
                          else [self.B_x32[4 * q + i] for i in range(4)])
                    P.op("sp", lambda e, q=q: e.dma_start(out=dst[:, 4 * q:4 * q + 4, :],
                                                          in_=xin[:, 4 * q:4 * q + 4, g * GT:(g + 1) * GT]),
                         writes=wr, dkey="xin%d" % q)
            self.prefetch_next = None
            for gi, g in enumerate(self.groups):
                staged = gi > 0
                if not staged:
                    load_x(g, False)
                if gi + 1 < len(self.groups):
                    gn = self.groups[gi + 1]
                    self.prefetch_next = lambda gn=gn: load_x(gn, True)
                self.c0 = 0
                self.nb = 2
                self.n = GT // 2
                if staged:
                    src = [(self.big[:, 2 * j * GT:(2 * j + 2) * GT].bitcast(F32), [self.B_big[2 * j], self.B_big[2 * j + 1]])
                           for j in range(16)]
                    self.ln_x("ln_in_g", "ln_in_b", src=src)
                else:
                    self.ln_x("ln_in_g", "ln_in_b")
                for l in range(self.depth):
                    self.layer(g, l)
                self.flush_deferred()
                if g == 0:
                    o = P.op("sp", lambda e: e.dma_start(out=oo[:, :, 0:GT - HALO], in_=x3[:, :, HALO:GT]),
                             reads=self.B_x32, dkey="out0")
                else:
                    o = P.op("sp", lambda e: e.dma_start(out=oo[:, :, GT - HALO:OWN], in_=x3[:, :, 0:GT]),
                             reads=self.B_x32, dkey="out1")
                finals.append(o)
            assert self.w_pos == len(self.wseq), (self.w_pos, len(self.wseq))
            P.emit(final_waits=finals)
        return nc


def _fm(v, nt):
    return np.ascontiguousarray(np.asarray(v, np.float32).reshape(nt, 128).T)


def _prep_shared(inp):
    f32 = np.float32
    w_in = np.asarray(inp["w_in"], f32)
    L = DEPTH
    win = np.ascontiguousarray(w_in.reshape(L, 16, 128, 104, 128).transpose(0, 3, 2, 1, 4)).reshape(L, 104, 128, 2048)
    wv = w_in[:, :, 4096:5120]
    wvv = np.ascontiguousarray(wv.reshape(L, 8, 2, 128, 1024).transpose(0, 1, 3, 2, 4)).reshape(L, 8, 128, 2048)
    w_br = np.asarray(inp["w_branch"], f32)
    wbr = np.ascontiguousarray(w_br.reshape(L, 3, 8, 128, 16, 128).transpose(0, 1, 4, 3, 2, 5)).reshape(L, 48, 128, 1024)
    w_out = np.asarray(inp["w_out"], f32)
    wout = np.ascontiguousarray(w_out.reshape(L, 16, 128, 16, 128).transpose(0, 3, 2, 1, 4)).reshape(L, 16, 128, 2048)
    w_fi = np.asarray(inp["w_ffn_in"], f32)
    t = w_fi.reshape(L, 16, 128, 2, 44, 128).transpose(0, 4, 3, 2, 1, 5)
    wfi = np.ascontiguousarray(t).reshape(L, 88, 128, 2048)
    w_fo = np.asarray(inp["w_ffn_out"], f32)
    wfo = np.ascontiguousarray(w_fo.reshape(L, 44, 128, 16, 128).transpose(0, 3, 2, 1, 4)).reshape(L, 16, 128, 5632)
    pp = np.zeros((128, NPP), f32)

    def put(key, arr):
        c = PPM[key]
        pp[:, c:c + arr.shape[1]] = arr
    put("ln_in_g", _fm(inp["ln_in_g"], 16))
    put("ln_in_b", _fm(inp["ln_in_b"], 16))
    for l in range(L):
        put(("gate_bias", l), _fm(np.asarray(inp["gate_bias"])[l], 48))
        ca = np.asarray(inp["conv_a_w"], f32)[l]
        put(("conv_a_w", l), np.ascontiguousarray(ca.reshape(3, 8, 128).transpose(2, 1, 0)).reshape(128, 24))
        put(("sg_ln_g", l), _fm(np.asarray(inp["sg_ln_g"])[l], 8))
        put(("sg_ln_b", l), _fm(np.asarray(inp["sg_ln_b"])[l], 8))
        cw = np.asarray(inp["cc_conv_w"], f32)[l]
        put(("cc_conv_w", l), np.ascontiguousarray(cw.reshape(31, 8, 128).transpose(2, 1, 0)).reshape(128, 248))
        put(("cc_conv_b", l), _fm(np.asarray(inp["cc_conv_b"])[l], 8))
        put(("cc_ln_g", l), _fm(np.asarray(inp["cc_ln_g"])[l], 8))
        put(("cc_ln_b", l), _fm(np.asarray(inp["cc_ln_b"])[l], 8))
        put(("ln_mix_g", l), _fm(np.asarray(inp["ln_mix_g"])[l], 16))
        put(("ln_mix_b", l), _fm(np.asarray(inp["ln_mix_b"])[l], 16))
        put(("ln_ffn_g", l), _fm(np.asarray(inp["ln_ffn_g"])[l], 16))
        put(("ln_ffn_b", l), _fm(np.asarray(inp["ln_ffn_b"])[l], 16))
    sg_w = np.asarray(inp["sg_w"], f32)
    sgwT = np.ascontiguousarray(sg_w.transpose(0, 3, 1, 2)).reshape(L, 128, 1024)
    sgb = np.ascontiguousarray(np.asarray(inp["sg_b"], f32).reshape(L, 1024))
    tri = np.concatenate([np.triu(np.ones((128, 128), f32)), np.eye(128, dtype=f32)], axis=1)
    return {"win": win, "wvv": wvv, "wbr": wbr, "wout": wout, "wfi": wfi, "wfo": wfo, "pp": pp,
            "sgwT": sgwT, "sgb": sgb, "tri": tri}


def _prep_core(x, c):
    b, q = divmod(c, 4)
    t0 = q * OWN
    xs = np.zeros((TL, D_MODEL), np.float32)
    lo = t0 - HALO
    if lo >= 0:
        xs[:] = x[b, lo:t0 + OWN]
        tm = np.ones((128, HALO), np.float32)
    else:
        xs[HALO:] = x[b, 0:OWN]
        tm = np.zeros((128, HALO), np.float32)
    xT = np.ascontiguousarray(xs.T).reshape(16, 128, TL)
    return {"xT": xT, "tmask": tm}


def kernel(**inputs):
    x = np.asarray(inputs["x"], np.float32)
    shared = _prep_shared(inputs)
    in_maps = []
    for c in range(NCORES):
        m = dict(shared)
        m.update(_prep_core(x, c))
        in_maps.append(m)
    nc = KB().build()
    res = run_bass_kernel_spmd(nc, in_maps, core_ids=list(range(NCORES)))
    out = np.empty((2, SEQ, D_MODEL), np.float32)
    for c in range(NCORES):
        b, q = divmod(c, 4)
        o = np.asarray(res.results[c]["out"]).reshape(D_MODEL, OWN)
        out[b, q * OWN:(q + 1) * OWN, :] = o.T
    return out
```
